# Optimizing a Trainium2 kernel written in Bass

```python
import math
import jax
import jax.numpy as jnp
from jax import lax
import numpy as np

D_MODEL = 1024
BATCH = 8
SEQ = 8192
DEPTH = 1

GRID_W = 64
N_META = 16
NA_HEADS = 8
NA_HEAD_DIM = 64
NA_WIDTH = NA_HEADS * NA_HEAD_DIM
WIN_ROWS = 8
WIN_COLS = 16
DN_HEADS = 4
DN_KEY_DIM = 128
DN_VAL_DIM = 128
DN_QK_WIDTH = DN_HEADS * DN_KEY_DIM
DN_V_WIDTH = DN_HEADS * DN_VAL_DIM
CONV_W = 5
CHUNK = 64
MIX_WIDTH = NA_WIDTH + DN_V_WIDTH
D_FF = ((8 * D_MODEL + 3 * 256 - 1) // (3 * 256)) * 256
IN_SPLITS = (NA_WIDTH, NA_WIDTH, NA_WIDTH,
             DN_QK_WIDTH, DN_QK_WIDTH, DN_V_WIDTH, DN_V_WIDTH,
             2 * DN_HEADS, 2 * DN_HEADS)
IN_WIDTH = sum(IN_SPLITS)
EPS = 1e-6
NEG = -1e30

kernel_name = 'hybrid_natten_gated_deltanet_block'


def rmsnorm(x, g):
    x32 = x.astype(jnp.float32)
    y = x32 * lax.rsqrt(jnp.mean(x32 * x32, axis=-1, keepdims=True) + EPS)
    return (y * g.astype(jnp.float32)).astype(x.dtype)


def l2norm(x):
    x32 = x.astype(jnp.float32)
    return x32 * lax.rsqrt(jnp.sum(x32 * x32, axis=-1, keepdims=True) + EPS)


def split_points():
    pts, acc = [], 0
    for s in IN_SPLITS[:-1]:
        acc += s
        pts.append(acc)
    return pts


def neighbourhood_attention(q, k, v, rel_bias, rows):
    B, L, _ = q.shape
    H, dh = NA_HEADS, NA_HEAD_DIM
    kr, kc = min(WIN_ROWS, rows), WIN_COLS
    q = (q * (dh ** -0.5)).reshape(B, L, H, dh)
    k = k.reshape(B, L, H, dh)
    v = v.reshape(B, L, H, dh)
    qm, km, vm = q[:, :N_META], k[:, :N_META], v[:, :N_META]
    qg = q[:, N_META:].reshape(B, rows, GRID_W, H, dh)
    kg = k[:, N_META:].reshape(B, rows, GRID_W, H, dh)
    vg = v[:, N_META:].reshape(B, rows, GRID_W, H, dh)

    r = np.arange(rows)
    c = np.arange(GRID_W)
    key_rows = np.clip(r - kr // 2, 0, rows - kr)[:, None] + np.arange(kr)[None, :]
    col_start = np.clip(c - kc // 2, 0, GRID_W - kc)
    col_in = (c[None, :] >= col_start[:, None]) & (c[None, :] < col_start[:, None] + kc)
    dr_idx = key_rows - r[:, None] + (WIN_ROWS - 1)
    dc_idx = np.clip(c[None, :] - c[:, None], 1 - WIN_COLS, WIN_COLS - 1) + (WIN_COLS - 1)

    k_win = kg[:, key_rows]
    v_win = vg[:, key_rows].reshape(B, rows, kr * GRID_W, H, dh)

    bias = rel_bias.astype(jnp.float32)[:, dr_idx][:, :, :, dc_idx]
    bias = jnp.transpose(bias, (0, 1, 3, 2, 4))
    s_grid = jnp.einsum('brqhd,brikhd->bhrqik', qg, k_win).astype(jnp.float32) + bias
    s_grid = jnp.where(col_in[:, None, :], s_grid, NEG)
    s_meta = jnp.einsum('brqhd,bmhd->bhrqm', qg, km).astype(jnp.float32)
    logits = jnp.concatenate([s_grid.reshape(B, H, rows, GRID_W, kr * GRID_W), s_meta], axis=-1)
    p = jax.nn.softmax(logits, axis=-1).astype(v.dtype)
    o_grid = (jnp.einsum('bhrqn,brnhd->brqhd', p[..., :kr * GRID_W], v_win)
              + jnp.einsum('bhrqm,bmhd->brqhd', p[..., kr * GRID_W:], vm))
    o_grid = o_grid.reshape(B, rows * GRID_W, H * dh)

    pm = jax.nn.softmax(jnp.einsum('bmhd,bnhd->bhmn', qm, km).astype(jnp.float32), axis=-1).astype(v.dtype)
    o_meta = jnp.einsum('bhmn,bnhd->bmhd', pm, vm).reshape(B, N_META, H * dh)
    return jnp.concatenate([o_meta, o_grid], axis=1)


def short_conv(x, w):
    C = x.shape[-1]
    y = lax.conv_general_dilated(x, w[:, None, :].astype(x.dtype), window_strides=(1,),
                                 padding=[(CONV_W // 2, CONV_W // 2)],
                                 dimension_numbers=('NWC', 'WIO', 'NWC'),
                                 feature_group_count=C)
    return jax.nn.silu(y)


def to_chunks(x):
    B, T, H = x.shape[:3]
    x = x.reshape((B, T // CHUNK, CHUNK, H) + x.shape[3:])
    return jnp.moveaxis(x, 3, 1)


def chunked_gated_delta(q, k, v, g, beta):
    B, T, H, dk = q.shape
    dv = v.shape[-1]
    qc, kc_, vc = to_chunks(q), to_chunks(k), to_chunks(v)
    gc = jnp.cumsum(to_chunks(g), axis=-1)
    bc = to_chunks(beta)[..., None]
    kb = kc_ * bc
    vb = vc * bc
    incl = np.tril(np.ones((CHUNK, CHUNK), dtype=bool))
    strict = np.tril(np.ones((CHUNK, CHUNK), dtype=bool), -1)
    diff = gc[..., :, None] - gc[..., None, :]
    decay_mat = jnp.where(incl, jnp.exp(jnp.where(incl, diff, 0.0)), 0.0)
    m = jnp.where(strict, jnp.einsum('bhnid,bhnjd->bhnij', kb, kc_) * decay_mat, 0.0)
    a_mat = m + jnp.eye(CHUNK, dtype=jnp.float32)
    rhs = jnp.concatenate([vb, kb * jnp.exp(gc)[..., None]], axis=-1)
    sol = lax.linalg.triangular_solve(a_mat, rhs, left_side=True, lower=True, unit_diagonal=True)
    u = sol[..., :dv]
    w = sol[..., dv:]
    qk = jnp.einsum('bhnid,bhnjd->bhnij', qc, kc_) * decay_mat
    q_dec = qc * jnp.exp(gc)[..., None]
    k_dec = kc_ * jnp.exp(gc[..., -1:] - gc)[..., None]
    chunk_decay = jnp.exp(gc[..., -1])

    def step(state, xs):
        qk_i, q_i, w_i, u_i, k_i, cd_i = xs
        v_new = u_i - jnp.einsum('bhck,bhkv->bhcv', w_i, state)
        o_i = jnp.einsum('bhck,bhkv->bhcv', q_i, state) + jnp.einsum('bhij,bhjv->bhiv', qk_i, v_new)
        state = state * cd_i[..., None, None] + jnp.einsum('bhck,bhcv->bhkv', k_i, v_new)
        return state, o_i

    xs = tuple(jnp.moveaxis(t, 2, 0) for t in (qk, q_dec, w, u, k_dec, chunk_decay))
    s0 = jnp.zeros((B, H, dk, dv), jnp.float32)
    _, o = lax.scan(step, s0, xs)
    return jnp.transpose(o, (1, 0, 2, 3, 4)).transpose(0, 1, 3, 2, 4).reshape(B, T, H, dv)


def gated_deltanet(q, k, v, z, b, a, conv_w, a_log, dt_bias, norm_g):
    B, L, _ = q.shape
    H = DN_HEADS
    qkv = short_conv(jnp.concatenate([q, k, v], axis=-1), conv_w)
    q, k, v = jnp.split(qkv, [DN_QK_WIDTH, 2 * DN_QK_WIDTH], axis=-1)
    q = l2norm(q.reshape(B, L, H, DN_KEY_DIM)) * (DN_KEY_DIM ** -0.5)
    k = l2norm(k.reshape(B, L, H, DN_KEY_DIM))
    v = v.reshape(B, L, H, DN_VAL_DIM).astype(jnp.float32)
    beta = jax.nn.sigmoid(b.astype(jnp.float32)).reshape(B, L, 2, H)
    g = -jnp.exp(a_log.astype(jnp.float32)) * jax.nn.softplus(
        a.astype(jnp.float32).reshape(B, L, 2, H) + dt_bias.astype(jnp.float32))
    pad = (-L) % CHUNK

    def padf(t):
        return jnp.pad(t, [(0, 0), (pad, 0)] + [(0, 0)] * (t.ndim - 2))

    qp, kp, vp, gp, bp = padf(q), padf(k), padf(v), padf(g), padf(beta)
    o_fwd = chunked_gated_delta(qp, kp, vp, gp[:, :, 0], bp[:, :, 0])
    o_bwd = jnp.flip(chunked_gated_delta(jnp.flip(qp, 1), jnp.flip(kp, 1), jnp.flip(vp, 1),
                                         jnp.flip(gp[:, :, 1], 1), jnp.flip(bp[:, :, 1], 1)), 1)
    o = (o_fwd + o_bwd)[:, pad:]
    o = rmsnorm(o, norm_g) * jax.nn.silu(z.astype(jnp.float32).reshape(B, L, H, DN_VAL_DIM))
    return o.reshape(B, L, DN_V_WIDTH).astype(z.dtype)


def setup_inputs(seed: int = 0) -> dict:
    key = jax.random.key(seed)
    ks = jax.random.split(key, 16)
    f32 = jnp.float32

    def nrm(k, shape, scale):
        return jax.random.normal(k, shape, f32) * scale

    x = nrm(ks[0], (BATCH, SEQ, D_MODEL), 1.0)
    meta_tokens = nrm(ks[1], (N_META, D_MODEL), 1.0)
    g_mix = 1.0 + nrm(ks[2], (DEPTH, D_MODEL), 0.01)
    w_in = nrm(ks[3], (DEPTH, D_MODEL, IN_WIDTH), D_MODEL ** -0.5)
    na_rel_bias = nrm(ks[4], (DEPTH, NA_HEADS, 2 * WIN_ROWS - 1, 2 * WIN_COLS - 1), 0.02)
    dn_conv_w = nrm(ks[5], (DEPTH, CONV_W, 2 * DN_QK_WIDTH + DN_V_WIDTH), CONV_W ** -0.5)
    dn_a_log = jnp.log(jax.random.uniform(ks[6], (DEPTH, 2, DN_HEADS), f32, 1.0, 16.0))
    dt = jnp.exp(jax.random.uniform(ks[7], (DEPTH, 2, DN_HEADS), f32, math.log(1e-3), math.log(0.1)))
    dn_dt_bias = dt + jnp.log(-jnp.expm1(-dt))
    dn_norm_g = 1.0 + nrm(ks[8], (DEPTH, DN_VAL_DIM), 0.01)
    w_out = nrm(ks[9], (DEPTH, MIX_WIDTH, D_MODEL), MIX_WIDTH ** -0.5)
    g_ffn = 1.0 + nrm(ks[10], (DEPTH, D_MODEL), 0.01)
    w_gate = nrm(ks[11], (DEPTH, D_MODEL, D_FF), D_MODEL ** -0.5)
    w_up = nrm(ks[12], (DEPTH, D_MODEL, D_FF), D_MODEL ** -0.5)
    w_down = nrm(ks[13], (DEPTH, D_FF, D_MODEL), D_FF ** -0.5)
    g_final = 1.0 + nrm(ks[14], (D_MODEL,), 0.01)
    return {'x': x, 'meta_tokens': meta_tokens, 'g_mix': g_mix, 'w_in': w_in,
            'na_rel_bias': na_rel_bias, 'dn_conv_w': dn_conv_w, 'dn_a_log': dn_a_log,
            'dn_dt_bias': dn_dt_bias, 'dn_norm_g': dn_norm_g, 'w_out': w_out, 'g_ffn': g_ffn,
            'w_gate': w_gate, 'w_up': w_up, 'w_down': w_down, 'g_final': g_final}


def reference(x, meta_tokens, g_mix, w_in, na_rel_bias, dn_conv_w, dn_a_log, dn_dt_bias,
              dn_norm_g, w_out, g_ffn, w_gate, w_up, w_down, g_final):
    B, S, D = x.shape
    rows = S // GRID_W
    meta = jnp.broadcast_to(meta_tokens[None].astype(x.dtype), (B, N_META, D))
    h = jnp.concatenate([meta, x], axis=1)
    pts = split_points()
    for l in range(DEPTH):
        u = rmsnorm(h, g_mix[l])
        proj = u @ w_in[l]
        na_q, na_k, na_v, dn_q, dn_k, dn_v, dn_z, dn_b, dn_a = jnp.split(proj, pts, axis=-1)
        y_na = neighbourhood_attention(na_q, na_k, na_v, na_rel_bias[l], rows)
        y_dn = gated_deltanet(dn_q, dn_k, dn_v, dn_z, dn_b, dn_a, dn_conv_w[l], dn_a_log[l],
                              dn_dt_bias[l], dn_norm_g[l])
        h = h + jnp.concatenate([y_na, y_dn], axis=-1) @ w_out[l]
        u = rmsnorm(h, g_ffn[l])
        h = h + (jax.nn.silu(u @ w_gate[l]) * (u @ w_up[l])) @ w_down[l]
    return rmsnorm(h, g_final)[:, N_META:]
```

```python
import contextlib
import numpy as np
import concourse.bass as bass
import concourse.mybir as mybir
from concourse.bass_utils import run_bass_kernel_spmd

F32 = mybir.dt.float32
BF16 = mybir.dt.bfloat16
ALU = mybir.AluOpType
AF = mybir.ActivationFunctionType
AX = mybir.AxisListType

COMPUTE = ('pe', 'act', 'dve', 'pool')
NDMA = 6
D = 1024
DFF = 2816
NEGM = -30000.0
CFG = {'nopooldma': True, 'nopool': False, 'stop': -1, 'stop3': 99}


class V:
    def __init__(self, ap, sub):
        self.ap = ap
        self.sub = sub


def _ap(x):
    return x.ap if isinstance(x, V) else x


def _key(x):
    if isinstance(x, V):
        return (x.ap.name, x.sub)
    return (x.name, None)


def _num(x):
    return isinstance(x, (int, float))


class Prog:
    epoch = 0

    def __init__(self, nc):
        self.nc = nc
        self.q = {e: [] for e in ('pe', 'act', 'dve', 'pool', 'sp')}
        self.cnt = {e: 0 for e in COMPUTE}
        self.known = {e: {} for e in self.q}
        self.state = {}
        self.dma_n = {e: 0 for e in self.q}
        self.dma_tok = {e: {} for e in self.q}

    def _st(self, name):
        if name not in self.state:
            self.state[name] = {'whole': {'w': None, 'r': []}, 'subs': {}}
        return self.state[name]

    def _deps(self, reads, writes):
        toks = []
        for x in reads:
            name, sub = _key(x)
            st = self._st(name)
            if st['whole']['w']:
                toks.append(st['whole']['w'])
            if sub is None:
                for s in st['subs'].values():
                    if s['w']:
                        toks.append(s['w'])
            else:
                s = st['subs'].get(sub)
                if s and s['w']:
                    toks.append(s['w'])
        for x in writes:
            name, sub = _key(x)
            st = self._st(name)
            if st['whole']['w']:
                toks.append(st['whole']['w'])
            toks.extend(st['whole']['r'])
            if sub is None:
                for s in st['subs'].values():
                    if s['w']:
                        toks.append(s['w'])
                    toks.extend(s['r'])
            else:
                s = st['subs'].get(sub)
                if s:
                    if s['w']:
                        toks.append(s['w'])
                    toks.extend(s['r'])
        return toks

    @staticmethod
    def _compact(toks):
        best = {}
        for s, v in toks:
            if v > best.get(s, -1):
                best[s] = v
        return list(best.items())

    def _update(self, reads, writes, tok):
        for x in reads:
            name, sub = _key(x)
            st = self._st(name)
            if sub is None:
                st['whole']['r'].append(tok)
                if len(st['whole']['r']) > 12:
                    st['whole']['r'] = self._compact(st['whole']['r'])
            else:
                s = st['subs'].setdefault(sub, {'w': None, 'r': []})
                s['r'].append(tok)
                if len(s['r']) > 12:
                    s['r'] = self._compact(s['r'])
        for x in writes:
            name, sub = _key(x)
            st = self._st(name)
            if sub is None:
                st['whole'] = {'w': tok, 'r': []}
                st['subs'] = {}
            else:
                st['subs'][sub] = {'w': tok, 'r': []}

    def rec(self, eng, fn, reads, writes, dma=False):
        if eng == 'pool':
            if dma and CFG['nopooldma']:
                eng = 'sp'
            elif (not dma) and CFG['nopool']:
                eng = 'dve'
        toks = self._deps(reads, writes)
        need = {}
        for s, v in toks:
            if (not dma) and eng == 'pe' and s == 'pe':
                continue
            if v > need.get(s, -1):
                need[s] = v
        if dma:
            n = self.dma_n[eng]
            k = n % NDMA
            sem = f'{eng}_d{k}'
            prev = self.dma_tok[eng].get(k)
            if prev is not None and prev[1] > need.get(prev[0], -1):
                need[prev[0]] = prev[1]
            tok = (sem, 16 * (n // NDMA + 1))
            self.dma_n[eng] = n + 1
            self.dma_tok[eng][k] = tok
        else:
            self.cnt[eng] += 1
            tok = (eng, self.cnt[eng])
        kn = self.known[eng]
        waits = []
        for s, v in need.items():
            if kn.get(s, -1) >= v:
                continue
            kn[s] = v
            waits.append((s, v))
        self.q[eng].append((waits, fn, tok))
        self._update(reads, writes, tok)
        return tok

    def mm(self, out, lhsT, rhs, start=True, stop=True):
        o, l, r = _ap(out), _ap(lhsT), _ap(rhs)
        self.rec('pe', lambda e: e.matmul(o, l, r, start=start, stop=stop), [lhsT, rhs], [out])

    def transpose(self, out, in_, ident):
        o, i, d = _ap(out), _ap(in_), _ap(ident)
        self.rec('pe', lambda e: e.transpose(o, i, d), [in_, ident], [out])

    def act(self, out, in_, func, bias=None, scale=1.0, accum_out=None):
        o, i = _ap(out), _ap(in_)
        reads = [in_]
        kw = {}
        if bias is not None:
            if _num(bias):
                kw['bias'] = bias
            else:
                reads.append(bias)
                kw['bias'] = _ap(bias)
        if _num(scale):
            sc = scale
        else:
            reads.append(scale)
            sc = _ap(scale)
        writes = [out]
        if accum_out is not None:
            writes.append(accum_out)
            kw['accum_out'] = _ap(accum_out)
        self.rec('act', lambda e: e.activation(o, i, func, scale=sc, **kw), reads, writes)

    def tt(self, eng, out, in0, in1, op):
        o, a, b = _ap(out), _ap(in0), _ap(in1)
        self.rec(eng, lambda e: e.tensor_tensor(o, a, b, op), [in0, in1], [out])

    def ts(self, eng, out, in0, s1, s2=None, op0=ALU.mult, op1=ALU.bypass):
        o, a = _ap(out), _ap(in0)
        reads = [in0]
        if not _num(s1):
            reads.append(s1)
        if s2 is not None and not _num(s2):
            reads.append(s2)
        s1a = s1 if _num(s1) else _ap(s1)
        s2a = s2 if (s2 is None or _num(s2)) else _ap(s2)
        self.rec(eng, lambda e: e.tensor_scalar(o, a, s1a, s2a, op0, op1), reads, [out])

    def stt(self, out, in0, scalar, in1, op0, op1):
        o, a, b = _ap(out), _ap(in0), _ap(in1)
        reads = [in0, in1]
        if not _num(scalar):
            reads.append(scalar)
        sa = scalar if _num(scalar) else _ap(scalar)
        self.rec('dve', lambda e: e.scalar_tensor_tensor(o, a, sa, b, op0, op1), reads, [out])

    def copy(self, eng, out, in_):
        o, i = _ap(out), _ap(in_)
        if eng == 'act':
            self.rec(eng, lambda e: e.copy(o, i), [in_], [out])
        else:
            self.rec(eng, lambda e: e.tensor_copy(o, i), [in_], [out])

    def memset(self, eng, out, val):
        o = _ap(out)
        self.rec(eng, lambda e: e.memset(o, val), [], [out])

    def recip(self, out, in_):
        o, i = _ap(out), _ap(in_)
        self.rec('dve', lambda e: e.reciprocal(o, i), [in_], [out])

    def dma(self, eng, out, in_, **kw):
        o, i = _ap(out), _ap(in_)
        self.rec(eng, lambda e: e.dma_start(o, i, **kw), [in_], [out], dma=True)

    def emit(self, ges):
        nc = self.nc
        Prog.epoch += 1
        ep = Prog.epoch
        semnames = list(COMPUTE)
        for e in self.q:
            for k in range(min(NDMA, self.dma_n[e])):
                semnames.append(f'{e}_d{k}')
        sems = {s: ges.enter_context(nc.semaphore(f'{s}_e{ep}')) for s in semnames}
        final = []
        for e in COMPUTE:
            if self.cnt[e] > 0:
                final.append((e, self.cnt[e]))
        for e in self.q:
            for k, tok in self.dma_tok[e].items():
                final.append(tok)
        with nc.Block() as block:
            def run(engname):
                def f(eng):
                    for waits, fn, tok in self.q[engname]:
                        for s, v in waits:
                            eng.wait_ge(sems[s], v)
                        ins = fn(eng)
                        ins.then_inc(sems[tok[0]], 16 if '_d' in tok[0] else 1)
                    for s, v in final:
                        eng.wait_ge(sems[s], v)
                return f
            block.tensor(run('pe'))
            block.scalar(run('act'))
            block.vector(run('dve'))
            block.gpsimd(run('pool'))
            block.sync(run('sp'))


class Rot:
    def __init__(self, items):
        self.items = items
        self.i = 0

    def next(self):
        x = self.items[self.i % len(self.items)]
        self.i += 1
        return x


def host_consts(ntg):
    c = {}
    c['ident'] = np.eye(128, dtype=np.float32)
    c['ident4'] = np.tile(np.eye(128, dtype=np.float32), (1, 4))
    c['ones'] = np.ones((128, 128), dtype=np.float32)
    t = np.arange(128)
    u_f = (t[:, None] <= t[None, :]).astype(np.float32)
    u_b = (t[:, None] >= t[None, :]).astype(np.float32)
    c['u0'] = u_f
    c['u1'] = u_b
    m_f = np.where(t[None, :] > t[:, None], 0.0, NEGM).astype(np.float32)
    m_b = np.where(t[None, :] < t[:, None], 0.0, NEGM).astype(np.float32)
    c['mk0'] = np.tile(m_f, (1, 4))
    c['mk1'] = np.tile(m_b, (1, 4))
    vm = np.zeros((128, 1), dtype=np.float32)
    vm[112:] = 1.0
    c['vmask'] = vm
    return c


def na_classes(ntg):
    def cls(j):
        if j == 0:
            return 0
        if j == 1:
            return 1
        if j == ntg - 2:
            return 3
        if j == ntg - 1:
            return 4
        return 2
    return cls


def na_keytiles(j, ntg):
    k0 = min(max(j - 2, 0), ntg - 5)
    return [k0 + i for i in range(5)]


def build_badd(rel_bias, ntg):
    rows = 2 * ntg
    H = 8
    reps = [0, 1, 2, ntg - 2, ntg - 1]
    cc = np.arange(64)
    cs = np.clip(cc - 8, 0, 48)
    col_in = (cc[None, :] >= cs[:, None]) & (cc[None, :] < cs[:, None] + 16)
    dc_idx = np.clip(cc[None, :] - cc[:, None], -15, 15) + 15
    out = np.full((5, 5, 128, H, 128), NEGM, dtype=np.float32)
    for ci, j in enumerate(reps):
        kts = na_keytiles(j, ntg)
        for si, kt in enumerate(kts):
            for a in range(2):
                kr = 2 * kt + a
                for b in range(2):
                    qr = 2 * j + b
                    s = min(max(qr - 4, 0), rows - 8)
                    if not (s <= kr < s + 8):
                        continue
                    dr = kr - qr + 7
                    g = rel_bias[:, dr, :][:, dc_idx]
                    g = np.transpose(g, (2, 0, 1))
                    blk = np.where(col_in.T[:, None, :], g, np.float32(NEGM))
                    out[ci, si, a * 64:(a + 1) * 64, :, b * 64:(b + 1) * 64] = blk
    return out.reshape(5, 5, 128, H * 128)


def build(ntg, debug=None):
    NT = ntg + 1
    NS = ntg // 4
    nc = bass.Bass("TRN2", target_bir_lowering=False)

    def din(name, shape, dt=F32):
        return nc.dram_tensor(name, shape, dt, kind="ExternalInput").ap()

    x = din("x", [ntg * 128, D])
    meta = din("meta", [16, D])
    w_in = din("w_in", [D, 3600])
    w_out = din("w_out", [D, D])
    w_gate = din("w_gate", [D, DFF])
    w_up = din("w_up", [D, DFF])
    w_down = din("w_down", [DFF, D])
    gmix = din("gmix", [128, 8])
    gffn = din("gffn", [128, 8])
    gfin = din("gfin", [128, D])
    ngrep = din("ngrep", [128, 512])
    convw = din("convw", [128, 12, 5])
    alog = din("alog", [128, 8])
    dtb = din("dtb", [128, 8])
    badd = din("badd", [5, 5, 128, 1024])
    out = nc.dram_tensor("out", [ntg * 128, D], F32, kind="ExternalOutput").ap()

    def scratch(name, shape, dt):
        kind = "ExternalOutput" if (debug and name in debug) else "Internal"
        return nc.dram_tensor(name, shape, dt, kind=kind).ap()

    naqT = scratch("naqT", [NT, 128, 4, 128], BF16)
    nakT = scratch("nakT", [NT, 128, 4, 128], BF16)
    navg = scratch("navg", [NT, 128, 8, 64], BF16)
    qkT = scratch("qkT", [NT, 128, 8, 128], BF16)
    kvtok = scratch("kvtok", [NT, 128, 8, 128], BF16)
    gbs = scratch("gbs", [NT, 128, 16], F32)
    zs = scratch("zs", [NT, 128, 512], F32)
    od = scratch("od", [2, NT, 128, 512], F32)
    ynaT = scratch("ynaT", [NT, 128, 4, 128], BF16)
    h1s = scratch("h1s", [NT, 128, D], F32)

    hc = host_consts(ntg)
    cdram = {k: nc.inline_tensor(v, "c_" + k).ap() for k, v in hc.items()}

    with contextlib.ExitStack() as ges:
        def gsb(name, shape, dt):
            return ges.enter_context(nc.sbuf_tensor(name, shape, dt))

        ident_f = gsb("ident_f", [128, 128], F32)
        ident_b = gsb("ident_b", [128, 128], BF16)
        ident4_f = gsb("ident4_f", [128, 512], F32)
        ident4_b = gsb("ident4_b", [128, 512], BF16)
        ones_f = gsb("ones_f", [128, 128], F32)
        ones_b = gsb("ones_b", [128, 128], BF16)
        u_f = [gsb("u0", [128, 128], F32), gsb("u1", [128, 128], F32)]
        mk_f = gsb("mk_f", [128, 512], F32)
        mk_b = [gsb("mk0", [128, 512], BF16), gsb("mk1", [128, 512], BF16)]
        vmask = gsb("vmask", [128, 1], F32)
        epsb = gsb("epsb", [128, 1], F32)
        oneb = gsb("oneb", [128, 1], F32)

        P = Prog(nc)
        P.dma('sp', ident_f[:], cdram['ident'])
        P.dma('sp', ident4_f[:], cdram['ident4'])
        P.dma('sp', ones_f[:], cdram['ones'])
        P.dma('sp', u_f[0][:], cdram['u0'])
        P.dma('sp', u_f[1][:], cdram['u1'])
        P.dma('sp', vmask[:], cdram['vmask'])
        P.copy('dve', ident_b[:], ident_f[:])
        P.copy('dve', ident4_b[:], ident4_f[:])
        P.copy('dve', ones_b[:], ones_f[:])
        for d in range(2):
            P.dma('sp', mk_f[:], cdram[f'mk{d}'])
            P.copy('dve', mk_b[d][:], mk_f[:])
        P.memset('dve', epsb[:], 1e-6)
        P.memset('dve', oneb[:], 1.0)
        P.emit(ges)

        G = dict(ident_b=ident_b, ident_f=ident_f, ident4_f=ident4_f, ident4_b=ident4_b, ones_f=ones_f,
                 ones_b=ones_b, u_f=u_f, mk_b=mk_b, vmask=vmask, epsb=epsb, oneb=oneb)

        phase1(nc, ges, G, ntg, x, meta, w_in, gmix, convw, alog, dtb,
               naqT, nakT, navg, qkT, kvtok, gbs, zs)
        if debug is None or 'ynaT' in debug or 'out' in debug:
            phase2(nc, ges, G, ntg, naqT, nakT, navg, badd, ynaT)
        if debug is None or 'od' in debug or 'out' in debug:
            phase3(nc, ges, G, ntg, qkT, kvtok, gbs, od)
        if debug is None or 'h1s' in debug or 'out' in debug:
            phase4a(nc, ges, G, ntg, x, od, zs, ynaT, ngrep, w_out, h1s)
        if debug is None or 'out' in debug:
            phase4b(nc, ges, G, ntg, h1s, gffn, gfin, w_gate, w_up, w_down, out)
    return nc


def rms_rstd(P, G, ssq, rstd, n, scale):
    P.act(rstd, ssq, AF.Sqrt, bias=G['epsb'][:n, 0:1] if n < 128 else G['epsb'][:, 0:1], scale=scale)
    P.recip(rstd, rstd)


def phase1(nc, ges, G, ntg, x, meta, w_in, gmix, convw, alog, dtb,
           naqT, nakT, navg, qkT, kvtok, gbs, zs):
    NT = ntg + 1
    NS = ntg // 4
    ident_b = G['ident_b']
    with contextlib.ExitStack() as es:
        def sb(name, shape, dt):
            return es.enter_context(nc.sbuf_tensor("p1_" + name, shape, dt))

        def ps(name, shape, dt):
            return es.enter_context(nc.psum_tensor("p1_" + name, shape, dt))

        P = Prog(nc)
        wbf = sb("wbf", [128, 8, 3600], BF16)
        wst = [sb(f"wst{i}", [128, 1200], F32) for i in range(2)]
        gm = sb("gm", [128, 8], F32)
        cw = sb("cw", [128, 12, 5], F32)
        dg = sb("dg", [128, 60, 128], BF16)
        nexpa = sb("nexpa", [128, 8], F32)
        dtbt = sb("dtbt", [128, 8], F32)
        xt = [sb(f"xt{i}", [128, D], F32) for i in range(2)]
        junk = sb("junk", [128, D], BF16)
        ssq = sb("ssq", [128, 1], F32)
        rstd = sb("rstd", [128, 1], F32)
        xn = sb("xn", [128, D], BF16)
        uT = sb("uT", [128, 8, 512], BF16)
        qk_tok = sb("qk_tok", [128, 1024], BF16)
        qkT_st = [sb(f"qkT_st{i}", [128, 8, 128], BF16) for i in range(2)]
        v_st = [sb(f"v_st{i}", [128, 512], BF16) for i in range(2)]
        z_st = [sb(f"z_st{i}", [128, 512], F32) for i in range(2)]
        gb = [sb(f"gb{i}", [128, 16], F32) for i in range(2)]
        tmp8 = sb("tmp8", [128, 8], F32)
        XT = [sb(f"XT{i}", [128, 12, 516], BF16) for i in range(2)]
        Y = [sb(f"Y{i}", [128, 512], F32) for i in range(2)]
        sq = [sb(f"sq{i}", [128, 512], BF16) for i in range(2)]
        rs = [sb(f"rs{i}", [128, 512], F32) for i in range(2)]
        QKn = sb("QKn", [128, 8, 512], BF16)
        VT = sb("VT", [128, 4, 512], BF16)
        KV_st = [sb(f"KV_st{i}", [128, 8, 128], BF16) for i in range(2)]

        pT = Rot([ps(f"pT{i}", [128, 8, 128], BF16) for i in range(2)])
        pG = Rot([ps(f"pG{i}", [128, 512], F32) for i in range(6)])

        P.dma('sp', gm[:], gmix)
        P.dma('sp', cw[:], convw)
        P.dma('sp', nexpa[:], alog)
        P.dma('sp', dtbt[:], dtb)
        P.act(nexpa[:], nexpa[:], AF.Exp)
        P.ts('dve', nexpa[:], nexpa[:], -1.0)
        n = 0
        for k in range(8):
            for pc in range(3):
                st = wst[n % 2]
                P.dma('sp' if n % 2 == 0 else 'pool', st[:], w_in[k * 128:(k + 1) * 128, pc * 1200:(pc + 1) * 1200])
                P.ts('dve' if n % 2 == 0 else 'pool', wbf[:, k, pc * 1200:(pc + 1) * 1200], st[:], gm[:, k:k + 1])
                n += 1
        for c in range(12):
            for j in range(5):
                P.ts('pool', dg[:, c * 5 + j, :], G['ident_f'][:], cw[:, c, j:j + 1])

        for b in range(2):
            P.memset('pool', XT[b][:], 0.0)

        groups = [[0]] + [[1 + 4 * s + i for i in range(4)] for s in range(NS)]
        cnt = [0]

        def evac(out_ap, in_ap):
            e = 'act' if cnt[0] % 2 == 0 else 'dve'
            cnt[0] += 1
            P.copy(e, out_ap, in_ap)

        def conv_group(gi):
            tiles = groups[gi]
            ncol = 128 * len(tiles)
            B = XT[gi % 2]
            for c in range(12):
                pc = pG.next()
                for j in range(5):
                    P.mm(pc[:, 0:ncol], dg[:, c * 5 + j, :], B[:, c, j:j + ncol], start=(j == 0), stop=(j == 4))
                y = Y[c % 2]
                P.act(y[:, 0:ncol], pc[:, 0:ncol], AF.Silu)
                if gi == 0:
                    P.memset('pool', y[:, 0:112], 0.0)
                if c < 8:
                    s_ = sq[c % 2]
                    r_ = rs[c % 2]
                    P.act(s_[:, 0:ncol], y[:, 0:ncol], AF.Square)
                    pn = pG.next()
                    P.mm(pn[:, 0:ncol], G['ones_b'][:], s_[:, 0:ncol])
                    P.act(r_[:, 0:ncol], pn[:, 0:ncol], AF.Sqrt, bias=G['epsb'][:, 0:1])
                    P.recip(r_[:, 0:ncol], r_[:, 0:ncol])
                    qs = (128.0 ** -0.5) if c < 4 else 1.0
                    P.stt(QKn[:, c, 0:ncol], y[:, 0:ncol], qs, r_[:, 0:ncol], ALU.mult, ALU.mult)
                else:
                    P.copy('pool', VT[:, c - 8, 0:ncol], y[:, 0:ncol])
            for i, t in enumerate(tiles):
                P.dma('sp', qkT[t], QKn[:, :, i * 128:(i + 1) * 128])
                pk = pT.next()
                for h in range(4):
                    P.transpose(pk[:, h, :], QKn[:, 4 + h, i * 128:(i + 1) * 128], ident_b[:])
                    P.transpose(pk[:, 4 + h, :], VT[:, h, i * 128:(i + 1) * 128], ident_b[:])
                kvs = KV_st[i % 2]
                evac(kvs[:], pk[:])
                P.dma('pool', kvtok[t], kvs[:])

        for gi, tiles in enumerate(groups):
            if CFG['stop'] >= 0 and gi >= CFG['stop']:
                break
            ncol = 128 * len(tiles)
            for i, t in enumerate(tiles):
                xb = xt[t % 2]
                if t == 0:
                    P.memset('pool', xb[:], 0.0)
                    P.dma('sp', xb[112:128, :], meta)
                else:
                    P.dma('sp', xb[:], x[(t - 1) * 128:t * 128, :])
                P.act(junk[:], xb[:], AF.Square, accum_out=ssq[:])
                rms_rstd(P, G, ssq[:], rstd[:], 128, 1.0 / D)
                P.ts('dve', xn[:], xb[:], rstd[:, 0:1])
                pt = pT.next()
                for c in range(8):
                    P.transpose(pt[:, c, :], xn[:, c * 128:(c + 1) * 128], ident_b[:])
                evac(uT[:, :, i * 128:(i + 1) * 128], pt[:])
                def tokproj(c0, nn):
                    pp = pG.next()
                    for k in range(8):
                        P.mm(pp[:, 0:nn], uT[:, k, i * 128:(i + 1) * 128], wbf[:, k, c0:c0 + nn],
                             start=(k == 0), stop=(k == 7))
                    return pp
                pq = tokproj(0, 512)
                P.act(qk_tok[:, 0:512], pq[:], AF.Copy, scale=0.125)
                pk = tokproj(512, 512)
                P.copy('dve', qk_tok[:, 512:1024], pk[:])
                pt2 = pT.next()
                for c in range(8):
                    P.transpose(pt2[:, c, :], qk_tok[:, c * 128:(c + 1) * 128], ident_b[:])
                qs_ = qkT_st[t % 2]
                evac(qs_[:], pt2[:])
                P.dma('sp', naqT[t], qs_[:, 0:4, :])
                P.dma('sp', nakT[t], qs_[:, 4:8, :])
                pv = tokproj(1024, 512)
                vs_ = v_st[t % 2]
                evac(vs_[:], pv[:])
                P.dma('pool', navg[t].rearrange("p h d -> p (h d)"), vs_[:])
                pz = tokproj(3072, 512)
                zt = z_st[t % 2]
                P.act(zt[:], pz[:], AF.Silu)
                P.dma('pool', zs[t], zt[:])
                pba = tokproj(3584, 16)
                g_ = gb[t % 2]
                P.act(g_[:, 8:16], pba[:, 0:8], AF.Sigmoid)
                P.tt('dve', tmp8[:], pba[:, 8:16], dtbt[:], ALU.add)
                P.act(tmp8[:], tmp8[:], AF.Exp)
                P.act(tmp8[:], tmp8[:], AF.Ln, bias=G['oneb'][:, 0:1])
                P.tt('dve', g_[:, 0:8], tmp8[:], nexpa[:], ALU.mult)
                if t == 0:
                    P.ts('dve', g_[:], g_[:], G['vmask'][:, 0:1])
                P.dma('sp', gbs[t], g_[:])
            B = XT[gi % 2]
            for c in range(12):
                pf = pG.next()
                for k in range(8):
                    P.mm(pf[:, 0:ncol], wbf[:, k, 1536 + c * 128:1536 + (c + 1) * 128], uT[:, k, 0:ncol],
                         start=(k == 0), stop=(k == 7))
                evac(B[:, c, 2:2 + ncol], pf[:, 0:ncol])
            P.memset('pool', B[:, :, 2 + ncol:4 + ncol], 0.0)
            if gi > 0:
                Bp = XT[(gi - 1) % 2]
                pcol = 128 * len(groups[gi - 1])
                P.copy('pool', Bp[:, :, 2 + pcol:4 + pcol], B[:, :, 2:4])
                P.copy('pool', B[:, :, 0:2], Bp[:, :, pcol:2 + pcol])
                conv_group(gi - 1)
            else:
                P.memset('pool', B[:, :, 0:2], 0.0)
        if CFG['stop'] < 0:
            conv_group(len(groups) - 1)
        P.emit(ges)


def phase2(nc, ges, G, ntg, naqT, nakT, navg, badd, ynaT):
    ident_b = G['ident_b']
    cls_of = na_classes(ntg)
    with contextlib.ExitStack() as es:
        def sb(name, shape, dt):
            return es.enter_context(nc.sbuf_tensor("p2_" + name, shape, dt))

        def ps(name, shape, dt):
            return es.enter_context(nc.psum_tensor("p2_" + name, shape, dt))

        P = Prog(nc)
        RING = 8
        kT = [sb(f"kT{i}", [128, 4, 128], BF16) for i in range(RING)]
        Va = [sb(f"Va{i}", [128, 8, 65], BF16) for i in range(RING)]
        kTm = sb("kTm", [128, 4, 128], BF16)
        Vm = sb("Vm", [128, 8, 65], BF16)
        qT = [sb(f"qT{i}", [128, 4, 128], BF16) for i in range(2)]
        qz = [sb(f"qz{i}", [128, 8, 128], BF16) for i in range(2)]
        bst = [sb(f"bst{i}", [128, 1024], F32) for i in range(2)]
        bad = [sb(f"bad{i}", [128, 1024], BF16) for i in range(5)]
        PTt = [sb(f"PT{i}", [128, 1024], BF16) for i in range(6)]
        rsum = sb("rsum", [128, 8], F32)
        yb = sb("yb", [128, 8, 64], BF16)
        yT = [sb(f"yT{i}", [128, 4, 128], BF16) for i in range(2)]

        pS = Rot([[ps(f"pS{i}_{k}", [128, 512], F32) for k in range(2)] for i in range(2)])
        pO = [ps(f"pO{k}", [128, 512], F32) for k in range(2)]
        pY = ps("pY", [128, 4, 128], BF16)

        for i in range(RING):
            P.memset('pool', Va[i][:, :, 64:65], 1.0)
        for i in range(2):
            P.memset('pool', qz[i][:], 0.0)
        P.dma('sp', kTm[:], nakT[0])
        P.dma('sp', Vm[:, :, 0:64], navg[0])
        for h in range(8):
            P.copy('pool', Vm[:, h, 64:65], G['vmask'][:, 0:1])
        loaded = {}
        cur_cls = [-1]

        def load_key(kt):
            slot = kt % RING
            if loaded.get(slot) == kt:
                return slot
            loaded[slot] = kt
            P.dma('sp', kT[slot][:], nakT[kt + 1])
            P.dma('pool', Va[slot][:, :, 0:64], navg[kt + 1])
            return slot

        for j in range(ntg):
            c = cls_of(j)
            if c != cur_cls[0]:
                cur_cls[0] = c
                for si in range(5):
                    st = bst[si % 2]
                    P.dma('sp', st[:], badd[c, si])
                    P.copy('pool', bad[si][:], st[:])
            q = qT[j % 2]
            z = qz[j % 2]
            P.dma('sp', q[:], naqT[j + 1])
            zv = z[:].rearrange("p (pr e) t -> p pr e t", e=2)
            P.copy('pool', zv[0:64, :, 0, :], q[0:64, :, :])
            P.copy('pool', zv[64:128, :, 1, :], q[64:128, :, :])
            kts = na_keytiles(j, ntg)
            slots = [load_key(kt) for kt in kts]
            for si in range(6):
                pS_ = pS.next()
                k_ = kT[slots[si]] if si < 5 else kTm
                for half in range(2):
                    if si < 5:
                        P.mm(pS_[half][:], ident_b[:], bad[si][:, half * 512:(half + 1) * 512],
                             start=True, stop=False)
                    for hl in range(4):
                        h = half * 4 + hl
                        if si < 5:
                            P.mm(pS_[half][:, hl * 128:(hl + 1) * 128], k_[:, h // 2, :], z[:, h, :],
                                 start=False, stop=(hl == 3))
                        else:
                            P.mm(pS_[half][:, hl * 128:(hl + 1) * 128], k_[:, h // 2, :], z[:, h, :],
                                 start=True, stop=True)
                for half in range(2):
                    P.act(PTt[si][:, half * 512:(half + 1) * 512], pS_[half][:], AF.Exp)
            for h in range(8):
                o_ = pO[h // 4][:, (h % 4) * 65:(h % 4) * 65 + 65]
                for si in range(6):
                    v_ = Va[slots[si]] if si < 5 else Vm
                    P.mm(o_, PTt[si][:, h * 128:(h + 1) * 128], v_[:, h, :], start=(si == 0), stop=(si == 5))
            for hf in range(2):
                ov = pO[hf][:, 0:260].rearrange("p (h e) -> p h e", e=65)
                P.recip(rsum[:, hf * 4:(hf + 1) * 4], ov[:, :, 64])
                for hl in range(4):
                    h = hf * 4 + hl
                    P.ts('dve', yb[:, h, :], ov[:, hl, 0:64], rsum[:, h:h + 1])
            ybf = yb[:].rearrange("p h e -> p (h e)")
            for pr in range(4):
                P.transpose(pY[:, pr, :], ybf[:, pr * 128:(pr + 1) * 128], ident_b[:])
            y_ = yT[j % 2]
            P.copy('act', y_[:], pY[:])
            P.dma('pool', ynaT[j + 1], y_[:])
        P.emit(ges)


def phase3(nc, ges, G, ntg, qkT, kvtok, gbs, od):
    NT = ntg + 1
    ident_b = G['ident_b']
    with contextlib.ExitStack() as es:
        def sb(name, shape, dt):
            return es.enter_context(nc.sbuf_tensor("p3_" + name, shape, dt))

        def ps(name, shape, dt):
            return es.enter_context(nc.psum_tensor("p3_" + name, shape, dt))

        P = Prog(nc)
        NB = 2
        qk = [[sb(f"qk{d}{i}", [128, 8, 128], BF16) for i in range(NB)] for d in range(2)]
        kv = [[sb(f"kv{d}{i}", [128, 8, 128], BF16) for i in range(NB)] for d in range(2)]
        gbt = [[sb(f"gb{d}{i}", [128, 16], F32) for i in range(NB)] for d in range(2)]
        sm = [[sb(f"sm{d}{i}", [128, 40], F32) for i in range(NB)] for d in range(2)]
        rhsg = [sb(f"rhsg{d}", [128, 4, 128], F32) for d in range(2)]
        DTs = [sb(f"DTs{d}", [128, 4, 128], F32) for d in range(2)]
        DTi = [sb(f"DTi{d}", [128, 4, 128], F32) for d in range(2)]
        qkTm = [[sb(f"qkTm{d}{i}", [128, 4, 128], BF16) for i in range(NB)] for d in range(2)]
        PTm = [[sb(f"PTm{d}{i}", [128, 4, 128], F32 if i < 2 else BF16) for i in range(3)] for d in range(2)]
        Am = [[sb(f"Am{d}{i}", [128, 4, 128], F32) for i in range(2)] for d in range(2)]
        Bm = [[sb(f"Bm{d}{i}", [128, 4, 128], F32) for i in range(2)] for d in range(2)]
        S = [sb(f"S{d}", [128, 4, 128], F32) for d in range(2)]
        Sb = [sb(f"Sb{d}", [128, 4, 128], BF16) for d in range(2)]
        R = [sb(f"R{d}", [128, 4, 128], BF16) for d in range(2)]
        vn = [sb(f"vn{d}", [128, 4, 128], BF16) for d in range(2)]
        vd = [sb(f"vd{d}", [128, 4, 128], BF16) for d in range(2)]
        tmpo = [sb(f"tmpo{d}", [128, 4, 128], F32) for d in range(2)]
        ob = [[sb(f"ob{d}{i}", [128, 4, 128], F32) for i in range(2)] for d in range(2)]

        pG = Rot([ps(f"pG{i}", [128, 4, 128], F32) for i in range(7)])
        pTr = ps("pTr", [128, 4, 128], F32)

        for d in range(2):
            P.memset('pool', S[d][:], 0.0)
            P.memset('pool', Sb[d][:], 0.0)

        def local(t, d, it):
            b = it % NB
            qk_, kv_, gb_, sm_ = qk[d][b], kv[d][b], gbt[d][b], sm[d][b]
            P.dma('sp', qk_[:], qkT[t])
            P.dma('pool', kv_[:], kvtok[t])
            P.dma('sp', gb_[:], gbs[t])
            gcol = gb_[:, d * 4:d * 4 + 4]
            bcol = gb_[:, 8 + d * 4:8 + d * 4 + 4]
            pg = pG.next()
            pgf = pg[:].rearrange("p h e -> p (h e)")
            P.mm(pgf[:, 0:4], G['u_f'][d][:], gcol, start=True, stop=True)
            P.mm(pgf[:, 4:8], G['ones_f'][:], gcol, start=True, stop=True)
            P.copy('dve', sm_[:, 0:8], pgf[:, 0:8])
            P.ts('dve', sm_[:, 8:12], sm_[:, 0:4], -1.0)
            P.act(sm_[:, 12:16], sm_[:, 0:4], AF.Exp)
            P.ts('dve', sm_[:, 16:20], sm_[:, 12:16], -1.0)
            P.tt('dve', sm_[:, 36:40], sm_[:, 4:8], sm_[:, 0:4], ALU.subtract)
            P.act(sm_[:, 20:24], sm_[:, 36:40], AF.Exp)
            P.act(sm_[:, 24:28], sm_[:, 4:8], AF.Exp)
            P.ts('dve', sm_[:, 28:32], bcol, -1.0)
            P.tt('dve', sm_[:, 32:36], bcol, sm_[:, 20:24], ALU.mult)
            if CFG['stop3'] <= 1:
                return None
            for h in range(4):
                P.ts('pool', rhsg[d][:, h, :], G['u_f'][d][:], gb_[:, d * 4 + h:d * 4 + h + 1])
            pb = pG.next()
            pbf = pb[:].rearrange("p h e -> p (h e)")
            P.mm(pbf, ident_b[:], G['mk_b'][d][:], start=True, stop=False)
            P.mm(pbf, G['ones_f'][:], rhsg[d][:].rearrange("p h e -> p (h e)"), start=False, stop=True)
            for h in range(4):
                P.act(DTs[d][:, h, :], pb[:, h, :], AF.Exp, bias=sm_[:, 8 + h:9 + h])
            P.tt('pool', DTi[d][:].rearrange("p h e -> p (h e)"), DTs[d][:].rearrange("p h e -> p (h e)"),
                 G['ident4_f'][:], ALU.add)
            if CFG['stop3'] <= 2:
                return None
            pkk = pG.next()
            pqk = pG.next()
            for h in range(4):
                P.mm(pkk[:, h, :], qk_[:, 4 + h, :], qk_[:, 4 + h, :])
            for h in range(4):
                P.mm(pqk[:, h, :], qk_[:, 4 + h, :], qk_[:, h, :])
            B0 = Bm[d][0]
            for h in range(4):
                P.stt(B0[:, h, :], pkk[:, h, :], sm_[:, 28 + h:29 + h], DTs[d][:, h, :], ALU.mult, ALU.mult)
            P.tt('dve', qkTm[d][b][:], pqk[:], DTi[d][:], ALU.mult)
            pt_i = 0
            PT = PTm[d][pt_i]
            P.tt('pool', PT[:].rearrange("p h e -> p (h e)"), B0[:].rearrange("p h e -> p (h e)"),
                 G['ident4_f'][:], ALU.add)
            if CFG['stop3'] <= 3:
                return None
            for h in range(4):
                P.transpose(pTr[:, h, :], B0[:, h, :], G['ident_f'][:])
            A0 = Am[d][0]
            P.copy('act', A0[:], pTr[:])
            Ac, Bc = A0, B0
            if CFG['stop3'] <= 4:
                return None
            for s in range(1, 7):
                An = Am[d][s % 2]
                Bn = Bm[d][s % 2]
                pa = pG.next()
                for h in range(4):
                    P.mm(pa[:, h, :], Bc[:, h, :], Ac[:, h, :])
                if s < 6:
                    pbb = pG.next()
                    for h in range(4):
                        P.mm(pbb[:, h, :], Ac[:, h, :], Bc[:, h, :])
                P.copy('act', An[:], pa[:])
                if s < 6:
                    P.copy('dve', Bn[:], pbb[:])
                pp = pG.next()
                for h in range(4):
                    P.mm(pp[:, h, :], An[:, h, :], PT[:, h, :])
                if s < 6:
                    PTn = PTm[d][(pt_i + 1) % 2]
                    pt_i = (pt_i + 1) % 2
                else:
                    PTn = PTm[d][2]
                P.tt('dve', PTn[:], pp[:], PT[:], ALU.add)
                PT = PTn
                Ac, Bc = An, Bn
            return PT

        def scan(t, d, it, PT):
            if PT is None or CFG['stop3'] <= 5:
                return
            b = it % NB
            qk_, kv_, gb_, sm_ = qk[d][b], kv[d][b], gbt[d][b], sm[d][b]
            bcol = gb_[:, 8 + d * 4:8 + d * 4 + 4]
            pks = pG.next()
            for h in range(4):
                P.mm(pks[:, h, :], qk_[:, 4 + h, :], Sb[d][:, h, :])
            for h in range(4):
                P.stt(R[d][:, h, :], pks[:, h, :], sm_[:, 16 + h:17 + h], kv_[:, 4 + h, :], ALU.mult, ALU.add)
            ptr = pG.next()
            for h in range(4):
                P.mm(ptr[:, h, :], PT[:, h, :], R[d][:, h, :])
            for h in range(4):
                P.ts('dve', vn[d][:, h, :], ptr[:, h, :], gb_[:, 8 + d * 4 + h:9 + d * 4 + h])
                P.ts('dve', vd[d][:, h, :], ptr[:, h, :], sm_[:, 32 + h:33 + h])
            pqs = pG.next()
            for h in range(4):
                P.mm(pqs[:, h, :], qk_[:, h, :], Sb[d][:, h, :])
            for h in range(4):
                P.ts('dve', tmpo[d][:, h, :], pqs[:, h, :], sm_[:, 12 + h:13 + h])
            po = pG.next()
            for h in range(4):
                P.mm(po[:, h, :], qkTm[d][b][:, h, :], vn[d][:, h, :])
            o_ = ob[d][it % 2]
            P.tt('dve', o_[:], po[:], tmpo[d][:], ALU.add)
            P.dma('sp', od[d, t], o_[:].rearrange("p h e -> p (h e)"))
            pss = pG.next()
            for h in range(4):
                P.mm(pss[:, h, :], kv_[:, h, :], vd[d][:, h, :])
            for h in range(4):
                P.stt(S[d][:, h, :], S[d][:, h, :], sm_[:, 24 + h:25 + h], pss[:, h, :], ALU.mult, ALU.add)
            P.copy('act', Sb[d][:], S[d][:])

        for n in range(NT):
            tf = n
            tb = NT - 1 - n
            PTf = local(tf, 0, n)
            PTb = local(tb, 1, n) if tb >= 1 else None
            scan(tf, 0, n, PTf)
            if tb >= 1:
                scan(tb, 1, n, PTb)
        P.emit(ges)


def phase4a(nc, ges, G, ntg, x, od, zs, ynaT, ngrep, w_out, h1s):
    ident_b = G['ident_b']
    with contextlib.ExitStack() as es:
        def sb(name, shape, dt):
            return es.enter_context(nc.sbuf_tensor("p4a_" + name, shape, dt))

        def ps(name, shape, dt):
            return es.enter_context(nc.psum_tensor("p4a_" + name, shape, dt))

        P = Prog(nc)
        wo = sb("wo", [128, 8, D], BF16)
        wst = [sb(f"wst{i}", [128, D], F32) for i in range(2)]
        ng = sb("ng", [128, 512], F32)
        xt = [sb(f"xt{i}", [128, D], F32) for i in range(2)]
        of = [sb(f"of{i}", [128, 4, 128], F32) for i in range(2)]
        obw = [sb(f"obw{i}", [128, 4, 128], F32) for i in range(2)]
        zt = [sb(f"zt{i}", [128, 4, 128], F32) for i in range(2)]
        yn = [sb(f"yn{i}", [128, 4, 128], BF16) for i in range(2)]
        junk = sb("junk", [128, 128], BF16)
        ssq = sb("ssq", [128, 4], F32)
        rstd = sb("rstd", [128, 4], F32)
        ydn = sb("ydn", [128, 4, 128], BF16)
        ydT = sb("ydT", [128, 4, 128], BF16)
        h1 = [sb(f"h1{i}", [128, D], F32) for i in range(2)]
        pY = ps("pY", [128, 4, 128], BF16)
        pH = Rot([ps(f"pH{i}", [128, 512], F32) for i in range(4)])

        P.dma('sp', ng[:], ngrep)
        for k in range(8):
            st = wst[k % 2]
            P.dma('sp', st[:], w_out[k * 128:(k + 1) * 128, :])
            P.copy('dve' if k % 2 == 0 else 'pool', wo[:, k, :], st[:])
        for t in range(1, ntg + 1):
            b = t % 2
            P.dma('sp', of[b][:].rearrange("p h e -> p (h e)"), od[0, t])
            P.dma('pool', obw[b][:].rearrange("p h e -> p (h e)"), od[1, t])
            P.dma('sp', zt[b][:].rearrange("p h e -> p (h e)"), zs[t])
            P.dma('pool', yn[b][:], ynaT[t])
            P.dma('sp', xt[b][:], x[(t - 1) * 128:t * 128, :])
            o = of[b]
            P.tt('pool', o[:], o[:], obw[b][:], ALU.add)
            for h in range(4):
                P.act(junk[:], o[:, h, :], AF.Square, accum_out=ssq[:, h:h + 1])
            P.act(rstd[:], ssq[:], AF.Sqrt, bias=G['epsb'][:, 0:1], scale=1.0 / 128)
            P.recip(rstd[:], rstd[:])
            P.tt('pool', zt[b][:].rearrange("p h e -> p (h e)"), zt[b][:].rearrange("p h e -> p (h e)"), ng[:], ALU.mult)
            for h in range(4):
                P.stt(ydn[:, h, :], o[:, h, :], rstd[:, h:h + 1], zt[b][:, h, :], ALU.mult, ALU.mult)
            for h in range(4):
                P.transpose(pY[:, h, :], ydn[:, h, :], ident_b[:])
            P.copy('act', ydT[:], pY[:])
            hb = h1[b]
            for half in range(2):
                ph = pH.next()
                for k in range(8):
                    l = yn[b][:, k, :] if k < 4 else ydT[:, k - 4, :]
                    P.mm(ph[:], l, wo[:, k, half * 512:(half + 1) * 512], start=(k == 0), stop=(k == 7))
                P.tt('dve', hb[:, half * 512:(half + 1) * 512], ph[:], xt[b][:, half * 512:(half + 1) * 512], ALU.add)
            P.dma('pool', h1s[t], hb[:])
        P.emit(ges)


def phase4b(nc, ges, G, ntg, h1s, gffn, gfin, w_gate, w_up, w_down, out):
    ident_b = G['ident_b']
    NS = ntg // 4
    NF = DFF // 128
    with contextlib.ExitStack() as es:
        def sb(name, shape, dt):
            return es.enter_context(nc.sbuf_tensor("p4b_" + name, shape, dt))

        def ps(name, shape, dt):
            return es.enter_context(nc.psum_tensor("p4b_" + name, shape, dt))

        P = Prog(nc)
        wg = sb("wg", [128, 8, DFF], BF16)
        wu = sb("wu", [128, 8, DFF], BF16)
        wd = sb("wd", [128, NF, D], BF16)
        wst = sb("wst", [128, 1408], F32)
        gf = sb("gf", [128, 8], F32)
        gfi = sb("gfi", [128, D], F32)
        h1 = [sb(f"h1{i}", [128, D], F32) for i in range(4)]
        junk = sb("junk", [128, D], BF16)
        ssq = sb("ssq", [128, 1], F32)
        rstd = sb("rstd", [128, 1], F32)
        hn = sb("hn", [128, D], BF16)
        hT = sb("hT", [128, 8, 512], BF16)
        sil = [sb(f"sil{i}", [128, 512], F32) for i in range(2)]
        actT = sb("actT", [128, NF, 512], BF16)
        pT = ps("pT", [128, 8, 128], BF16)
        pG = Rot([ps(f"pG{i}", [128, 512], F32) for i in range(6)])

        P.dma('sp', gf[:], gffn)
        P.dma('sp', gfi[:], gfin)
        n = 0
        for (wsrc, wdst) in ((w_gate, wg), (w_up, wu)):
            for k in range(8):
                for pc in range(2):
                    P.dma('sp' if n % 2 == 0 else 'pool', wst[:], wsrc[k * 128:(k + 1) * 128, pc * 1408:(pc + 1) * 1408])
                    P.ts('dve' if n % 2 == 0 else 'pool', wdst[:, k, pc * 1408:(pc + 1) * 1408], wst[:], gf[:, k:k + 1])
                    n += 1
        for f in range(NF):
            P.dma('sp' if n % 2 == 0 else 'pool', wst[:, 0:D], w_down[f * 128:(f + 1) * 128, :])
            P.copy('dve' if n % 2 == 0 else 'pool', wd[:, f, :], wst[:, 0:D])
            n += 1
        for s in range(NS):
            tiles = [1 + 4 * s + i for i in range(4)]
            for i, t in enumerate(tiles):
                hb = h1[i]
                P.dma('sp', hb[:], h1s[t])
                P.act(junk[:], hb[:], AF.Square, accum_out=ssq[:])
                rms_rstd(P, G, ssq[:], rstd[:], 128, 1.0 / D)
                P.ts('dve', hn[:], hb[:], rstd[:, 0:1])
                for c in range(8):
                    P.transpose(pT[:, c, :], hn[:, c * 128:(c + 1) * 128], ident_b[:])
                P.copy('act', hT[:, :, i * 128:(i + 1) * 128], pT[:])
            for f in range(NF):
                pg = pG.next()
                pu = pG.next()
                for k in range(8):
                    P.mm(pg[:], wg[:, k, f * 128:(f + 1) * 128], hT[:, k, :], start=(k == 0), stop=(k == 7))
                for k in range(8):
                    P.mm(pu[:], wu[:, k, f * 128:(f + 1) * 128], hT[:, k, :], start=(k == 0), stop=(k == 7))
                sl = sil[f % 2]
                P.act(sl[:], pg[:], AF.Silu)
                P.tt('dve', actT[:, f, :], pu[:], sl[:], ALU.mult)
            for i, t in enumerate(tiles):
                hb = h1[i]
                for half in range(2):
                    pd = pG.next()
                    for f in range(NF):
                        P.mm(pd[:], actT[:, f, i * 128:(i + 1) * 128], wd[:, f, half * 512:(half + 1) * 512],
                             start=(f == 0), stop=(f == NF - 1))
                    P.tt('dve', hb[:, half * 512:(half + 1) * 512], pd[:], hb[:, half * 512:(half + 1) * 512], ALU.add)
                P.act(junk[:], hb[:], AF.Square, accum_out=ssq[:])
                rms_rstd(P, G, ssq[:], rstd[:], 128, 1.0 / D)
                P.stt(hb[:], hb[:], rstd[:, 0:1], gfi[:], ALU.mult, ALU.mult)
                P.dma('pool', out[(t - 1) * 128:t * 128, :], hb[:])
        P.emit(ges)


def make_inmap(xb, inp, ntg):
    f = np.float32
    m = {}
    m['x'] = np.ascontiguousarray(xb, dtype=f)
    m['meta'] = np.ascontiguousarray(inp['meta_tokens'], dtype=f)
    m['w_in'] = np.ascontiguousarray(inp['w_in'][0], dtype=f)
    m['w_out'] = np.ascontiguousarray(inp['w_out'][0], dtype=f)
    m['w_gate'] = np.ascontiguousarray(inp['w_gate'][0], dtype=f)
    m['w_up'] = np.ascontiguousarray(inp['w_up'][0], dtype=f)
    m['w_down'] = np.ascontiguousarray(inp['w_down'][0], dtype=f)
    m['gmix'] = np.ascontiguousarray(np.asarray(inp['g_mix'][0], dtype=f).reshape(8, 128).T)
    m['gffn'] = np.ascontiguousarray(np.asarray(inp['g_ffn'][0], dtype=f).reshape(8, 128).T)
    m['gfin'] = np.ascontiguousarray(np.broadcast_to(np.asarray(inp['g_final'], dtype=f)[None, :], (128, D)))
    ng = np.asarray(inp['dn_norm_g'][0], dtype=f)
    m['ngrep'] = np.ascontiguousarray(np.broadcast_to(np.tile(ng, 4)[None, :], (128, 512)))
    cw = np.asarray(inp['dn_conv_w'][0], dtype=f)
    m['convw'] = np.ascontiguousarray(cw.T.reshape(12, 128, 5).transpose(1, 0, 2))
    m['alog'] = np.ascontiguousarray(np.broadcast_to(np.asarray(inp['dn_a_log'][0], dtype=f).reshape(1, 8), (128, 8)))
    m['dtb'] = np.ascontiguousarray(np.broadcast_to(np.asarray(inp['dn_dt_bias'][0], dtype=f).reshape(1, 8), (128, 8)))
    m['badd'] = build_badd(np.asarray(inp['na_rel_bias'][0], dtype=f), ntg)
    return m


_NC_CACHE = {}


def kernel(**inputs):
    x = np.asarray(inputs['x'])
    B, S, _ = x.shape
    ntg = S // 128
    if ntg not in _NC_CACHE:
        _NC_CACHE[ntg] = build(ntg)
    nc = _NC_CACHE[ntg]
    base = make_inmap(x[0], inputs, ntg)
    in_maps = []
    for b in range(B):
        m = dict(base)
        m['x'] = np.ascontiguousarray(x[b], dtype=np.float32)
        in_maps.append(m)
    res = run_bass_kernel_spmd(nc, in_maps, core_ids=list(range(B)))
    return np.stack([np.asarray(r['out'], dtype=np.float32) for r in res.results], axis=0)
```

```python
import contextlib
import numpy as np
import concourse.bass as bass
import concourse.mybir as mybir
from concourse.bass_utils import run_bass_kernel_spmd

F32 = mybir.dt.float32
BF16 = mybir.dt.bfloat16
ALU = mybir.AluOpType
AF = mybir.ActivationFunctionType
AX = mybir.AxisListType

COMPUTE = ('pe', 'act', 'dve', 'pool')
NDMA = 6
D = 1024
DFF = 2816
NEGM = -30000.0
CFG = {'nopooldma': True, 'nopool': False, 'stop': -1, 'stop3': 99}


class V:
    def __init__(self, ap, sub):
        self.ap = ap
        self.sub = sub


def _ap(x):
    return x.ap if isinstance(x, V) else x


def _key(x):
    if isinstance(x, V):
        return (x.ap.name, x.sub)
    return (x.name, None)


def _num(x):
    return isinstance(x, (int, float))


class Prog:
    epoch = 0

    def __init__(self, nc):
        self.nc = nc
        self.q = {e: [] for e in ('pe', 'act', 'dve', 'pool', 'sp')}
        self.cnt = {e: 0 for e in COMPUTE}
        self.known = {e: {} for e in self.q}
        self.state = {}
        self.dma_n = {e: 0 for e in self.q}
        self.dma_tok = {e: {} for e in self.q}

    def _st(self, name):
        if name not in self.state:
            self.state[name] = {'whole': {'w': None, 'r': []}, 'subs': {}}
        return self.state[name]

    def _deps(self, reads, writes):
        toks = []
        for x in reads:
            name, sub = _key(x)
            st = self._st(name)
            if st['whole']['w']:
                toks.append(st['whole']['w'])
            if sub is None:
                for s in st['subs'].values():
                    if s['w']:
                        toks.append(s['w'])
            else:
                s = st['subs'].get(sub)
                if s and s['w']:
                    toks.append(s['w'])
        for x in writes:
            name, sub = _key(x)
            st = self._st(name)
            if st['whole']['w']:
                toks.append(st['whole']['w'])
            toks.extend(st['whole']['r'])
            if sub is None:
                for s in st['subs'].values():
                    if s['w']:
                        toks.append(s['w'])
                    toks.extend(s['r'])
            else:
                s = st['subs'].get(sub)
                if s:
                    if s['w']:
                        toks.append(s['w'])
                    toks.extend(s['r'])
        return toks

    @staticmethod
    def _compact(toks):
        best = {}
        for s, v in toks:
            if v > best.get(s, -1):
                best[s] = v
        return list(best.items())

    def _update(self, reads, writes, tok):
        for x in reads:
            name, sub = _key(x)
            st = self._st(name)
            if sub is None:
                st['whole']['r'].append(tok)
                if len(st['whole']['r']) > 12:
                    st['whole']['r'] = self._compact(st['whole']['r'])
            else:
                s = st['subs'].setdefault(sub, {'w': None, 'r': []})
                s['r'].append(tok)
                if len(s['r']) > 12:
                    s['r'] = self._compact(s['r'])
        for x in writes:
            name, sub = _key(x)
            st = self._st(name)
            if sub is None:
                st['whole'] = {'w': tok, 'r': []}
                st['subs'] = {}
            else:
                st['subs'][sub] = {'w': tok, 'r': []}

    def rec(self, eng, fn, reads, writes, dma=False):
        if eng == 'pool':
            if dma and CFG['nopooldma']:
                eng = 'sp'
            elif (not dma) and CFG['nopool']:
                eng = 'dve'
        toks = self._deps(reads, writes)
        need = {}
        for s, v in toks:
            if (not dma) and eng == 'pe' and s == 'pe':
                continue
            if v > need.get(s, -1):
                need[s] = v
        if dma:
            n = self.dma_n[eng]
            k = n % NDMA
            sem = f'{eng}_d{k}'
            prev = self.dma_tok[eng].get(k)
            if prev is not None and prev[1] > need.get(prev[0], -1):
                need[prev[0]] = prev[1]
            tok = (sem, 16 * (n // NDMA + 1))
            self.dma_n[eng] = n + 1
            self.dma_tok[eng][k] = tok
        else:
            self.cnt[eng] += 1
            tok = (eng, self.cnt[eng])
        kn = self.known[eng]
        waits = []
        for s, v in need.items():
            if kn.get(s, -1) >= v:
                continue
            kn[s] = v
            waits.append((s, v))
        self.q[eng].append((waits, fn, tok))
        self._update(reads, writes, tok)
        return tok

    def mm(self, out, lhsT, rhs, start=True, stop=True):
        o, l, r = _ap(out), _ap(lhsT), _ap(rhs)
        self.rec('pe', lambda e: e.matmul(o, l, r, start=start, stop=stop), [lhsT, rhs], [out])

    def transpose(self, out, in_, ident):
        o, i, d = _ap(out), _ap(in_), _ap(ident)
        self.rec('pe', lambda e: e.transpose(o, i, d), [in_, ident], [out])

    def act(self, out, in_, func, bias=None, scale=1.0, accum_out=None):
        o, i = _ap(out), _ap(in_)
        reads = [in_]
        kw = {}
        if bias is not None:
            if _num(bias):
                kw['bias'] = bias
            else:
                reads.append(bias)
                kw['bias'] = _ap(bias)
        if _num(scale):
            sc = scale
        else:
            reads.append(scale)
            sc = _ap(scale)
        writes = [out]
        if accum_out is not None:
            writes.append(accum_out)
            kw['accum_out'] = _ap(accum_out)
        self.rec('act', lambda e: e.activation(o, i, func, scale=sc, **kw), reads, writes)

    def tt(self, eng, out, in0, in1, op):
        o, a, b = _ap(out), _ap(in0), _ap(in1)
        self.rec(eng, lambda e: e.tensor_tensor(o, a, b, op), [in0, in1], [out])

    def ts(self, eng, out, in0, s1, s2=None, op0=ALU.mult, op1=ALU.bypass):
        o, a = _ap(out), _ap(in0)
        reads = [in0]
        if not _num(s1):
            reads.append(s1)
        if s2 is not None and not _num(s2):
            reads.append(s2)
        s1a = s1 if _num(s1) else _ap(s1)
        s2a = s2 if (s2 is None or _num(s2)) else _ap(s2)
        self.rec(eng, lambda e: e.tensor_scalar(o, a, s1a, s2a, op0, op1), reads, [out])

    def stt(self, out, in0, scalar, in1, op0, op1):
        o, a, b = _ap(out), _ap(in0), _ap(in1)
        reads = [in0, in1]
        if not _num(scalar):
            reads.append(scalar)
        sa = scalar if _num(scalar) else _ap(scalar)
        self.rec('dve', lambda e: e.scalar_tensor_tensor(o, a, sa, b, op0, op1), reads, [out])

    def copy(self, eng, out, in_):
        o, i = _ap(out), _ap(in_)
        if eng == 'act':
            self.rec(eng, lambda e: e.copy(o, i), [in_], [out])
        else:
            self.rec(eng, lambda e: e.tensor_copy(o, i), [in_], [out])

    def memset(self, eng, out, val):
        o = _ap(out)
        self.rec(eng, lambda e: e.memset(o, val), [], [out])

    def recip(self, out, in_):
        o, i = _ap(out), _ap(in_)
        self.rec('dve', lambda e: e.reciprocal(o, i), [in_], [out])

    def dma(self, eng, out, in_, **kw):
        o, i = _ap(out), _ap(in_)
        self.rec(eng, lambda e: e.dma_start(o, i, **kw), [in_], [out], dma=True)

    def emit(self, ges):
        nc = self.nc
        Prog.epoch += 1
        ep = Prog.epoch
        semnames = list(COMPUTE)
        for e in self.q:
            for k in range(min(NDMA, self.dma_n[e])):
                semnames.append(f'{e}_d{k}')
        sems = {s: ges.enter_context(nc.semaphore(f'{s}_e{ep}')) for s in semnames}
        final = []
        for e in COMPUTE:
            if self.cnt[e] > 0:
                final.append((e, self.cnt[e]))
        for e in self.q:
            for k, tok in self.dma_tok[e].items():
                final.append(tok)
        with nc.Block() as block:
            def run(engname):
                def f(eng):
                    for waits, fn, tok in self.q[engname]:
                        for s, v in waits:
                            eng.wait_ge(sems[s], v)
                        ins = fn(eng)
                        ins.then_inc(sems[tok[0]], 16 if '_d' in tok[0] else 1)
                    for s, v in final:
                        eng.wait_ge(sems[s], v)
                return f
            block.tensor(run('pe'))
            block.scalar(run('act'))
            block.vector(run('dve'))
            block.gpsimd(run('pool'))
            block.sync(run('sp'))


class Rot:
    def __init__(self, items):
        self.items = items
        self.i = 0

    def next(self):
        x = self.items[self.i % len(self.items)]
        self.i += 1
        return x


def host_consts(ntg):
    c = {}
    c['ident'] = np.eye(128, dtype=np.float32)
    c['ident4'] = np.tile(np.eye(128, dtype=np.float32), (1, 4))
    c['ones'] = np.ones((128, 128), dtype=np.float32)
    t = np.arange(128)
    u_f = (t[:, None] <= t[None, :]).astype(np.float32)
    u_b = (t[:, None] >= t[None, :]).astype(np.float32)
    c['u0'] = u_f
    c['u1'] = u_b
    m_f = np.where(t[None, :] > t[:, None], 0.0, NEGM).astype(np.float32)
    m_b = np.where(t[None, :] < t[:, None], 0.0, NEGM).astype(np.float32)
    c['mk0'] = np.tile(m_f, (1, 4))
    c['mk1'] = np.tile(m_b, (1, 4))
    vm = np.zeros((128, 1), dtype=np.float32)
    vm[112:] = 1.0
    c['vmask'] = vm
    return c


def na_classes(ntg):
    def cls(j):
        if j == 0:
            return 0
        if j == 1:
            return 1
        if j == ntg - 2:
            return 3
        if j == ntg - 1:
            return 4
        return 2
    return cls


def na_keytiles(j, ntg):
    k0 = min(max(j - 2, 0), ntg - 5)
    return [k0 + i for i in range(5)]


def build_badd(rel_bias, ntg):
    rows = 2 * ntg
    H = 8
    reps = [0, 1, 2, ntg - 2, ntg - 1]
    cc = np.arange(64)
    cs = np.clip(cc - 8, 0, 48)
    col_in = (cc[None, :] >= cs[:, None]) & (cc[None, :] < cs[:, None] + 16)
    dc_idx = np.clip(cc[None, :] - cc[:, None], -15, 15) + 15
    out = np.full((5, 5, 128, H, 128), NEGM, dtype=np.float32)
    for ci, j in enumerate(reps):
        kts = na_keytiles(j, ntg)
        for si, kt in enumerate(kts):
            for a in range(2):
                kr = 2 * kt + a
                for b in range(2):
                    qr = 2 * j + b
                    s = min(max(qr - 4, 0), rows - 8)
                    if not (s <= kr < s + 8):
                        continue
                    dr = kr - qr + 7
                    g = rel_bias[:, dr, :][:, dc_idx]
                    g = np.transpose(g, (2, 0, 1))
                    blk = np.where(col_in.T[:, None, :], g, np.float32(NEGM))
                    out[ci, si, a * 64:(a + 1) * 64, :, b * 64:(b + 1) * 64] = blk
    return out.reshape(5, 5, 128, H * 128)


def build(ntg, debug=None):
    NT = ntg + 1
    NS = ntg // 4
    nc = bass.Bass("TRN2", target_bir_lowering=False)

    def din(name, shape, dt=F32):
        return nc.dram_tensor(name, shape, dt, kind="ExternalInput").ap()

    x = din("x", [ntg * 128, D])
    meta = din("meta", [16, D])
    w_in = din("w_in", [D, 3600])
    w_out = din("w_out", [D, D])
    w_gate = din("w_gate", [D, DFF])
    w_up = din("w_up", [D, DFF])
    w_down = din("w_down", [DFF, D])
    gmix = din("gmix", [128, 8])
    gffn = din("gffn", [128, 8])
    gfin = din("gfin", [128, D])
    ngrep = din("ngrep", [128, 512])
    convw = din("convw", [128, 12, 5])
    alog = din("alog", [128, 8])
    dtb = din("dtb", [128, 8])
    badd = din("badd", [5, 5, 128, 1024])
    out = nc.dram_tensor("out", [ntg * 128, D], F32, kind="ExternalOutput").ap()

    def scratch(name, shape, dt):
        kind = "ExternalOutput" if (debug and name in debug) else "Internal"
        return nc.dram_tensor(name, shape, dt, kind=kind).ap()

    naqT = scratch("naqT", [NT, 128, 4, 128], BF16)
    nakT = scratch("nakT", [NT, 128, 4, 128], BF16)
    navg = scratch("navg", [NT, 128, 8, 64], BF16)
    qkT = scratch("qkT", [NT, 128, 8, 128], BF16)
    kvtok = scratch("kvtok", [NT, 128, 8, 128], BF16)
    gbs = scratch("gbs", [NT, 128, 16], F32)
    zs = scratch("zs", [NT, 128, 512], F32)
    od = scratch("od", [2, NT, 128, 512], F32)
    ynaT = scratch("ynaT", [NT, 128, 4, 128], BF16)
    h1s = scratch("h1s", [NT, 128, D], F32)

    hc = host_consts(ntg)
    cdram = {k: nc.inline_tensor(v, "c_" + k).ap() for k, v in hc.items()}

    with contextlib.ExitStack() as ges:
        def gsb(name, shape, dt):
            return ges.enter_context(nc.sbuf_tensor(name, shape, dt))

        ident_f = gsb("ident_f", [128, 128], F32)
        ident_b = gsb("ident_b", [128, 128], BF16)
        ident4_f = gsb("ident4_f", [128, 512], F32)
        ident4_b = gsb("ident4_b", [128, 512], BF16)
        ones_f = gsb("ones_f", [128, 128], F32)
        ones_b = gsb("ones_b", [128, 128], BF16)
        u_f = [gsb("u0", [128, 128], F32), gsb("u1", [128, 128], F32)]
        mk_f = gsb("mk_f", [128, 512], F32)
        mk_b = [gsb("mk0", [128, 512], BF16), gsb("mk1", [128, 512], BF16)]
        vmask = gsb("vmask", [128, 1], F32)
        epsb = gsb("epsb", [128, 1], F32)
        oneb = gsb("oneb", [128, 1], F32)

        P = Prog(nc)
        P.dma('sp', ident_f[:], cdram['ident'])
        P.dma('sp', ident4_f[:], cdram['ident4'])
        P.dma('sp', ones_f[:], cdram['ones'])
        P.dma('sp', u_f[0][:], cdram['u0'])
        P.dma('sp', u_f[1][:], cdram['u1'])
        P.dma('sp', vmask[:], cdram['vmask'])
        P.copy('dve', ident_b[:], ident_f[:])
        P.copy('dve', ident4_b[:], ident4_f[:])
        P.copy('dve', ones_b[:], ones_f[:])
        for d in range(2):
            P.dma('sp', mk_f[:], cdram[f'mk{d}'])
            P.copy('dve', mk_b[d][:], mk_f[:])
        P.memset('dve', epsb[:], 1e-6)
        P.memset('dve', oneb[:], 1.0)
        P.emit(ges)

        G = dict(ident_b=ident_b, ident_f=ident_f, ident4_f=ident4_f, ident4_b=ident4_b, ones_f=ones_f,
                 ones_b=ones_b, u_f=u_f, mk_b=mk_b, vmask=vmask, epsb=epsb, oneb=oneb)

        phase1(nc, ges, G, ntg, x, meta, w_in, gmix, convw, alog, dtb,
               naqT, nakT, navg, qkT, kvtok, gbs, zs)
        if debug is None or 'ynaT' in debug or 'out' in debug:
            phase2(nc, ges, G, ntg, naqT, nakT, navg, badd, ynaT)
        if debug is None or 'od' in debug or 'out' in debug:
            phase3(nc, ges, G, ntg, qkT, kvtok, gbs, od)
        if debug is None or 'h1s' in debug or 'out' in debug:
            phase4a(nc, ges, G, ntg, x, od, zs, ynaT, ngrep, w_out, h1s)
        if debug is None or 'out' in debug:
            phase4b(nc, ges, G, ntg, h1s, gffn, gfin, w_gate, w_up, w_down, out)
    return nc


def rms_rstd(P, G, ssq, rstd, n, scale):
    P.act(rstd, ssq, AF.Sqrt, bias=G['epsb'][:n, 0:1] if n < 128 else G['epsb'][:, 0:1], scale=scale)
    P.recip(rstd, rstd)


def phase1(nc, ges, G, ntg, x, meta, w_in, gmix, convw, alog, dtb,
           naqT, nakT, navg, qkT, kvtok, gbs, zs):
    NT = ntg + 1
    NS = ntg // 4
    ident_b = G['ident_b']
    with contextlib.ExitStack() as es:
        def sb(name, shape, dt):
            return es.enter_context(nc.sbuf_tensor("p1_" + name, shape, dt))

        def ps(name, shape, dt):
            return es.enter_context(nc.psum_tensor("p1_" + name, shape, dt))

        P = Prog(nc)
        wbf = sb("wbf", [128, 8, 3600], BF16)
        wst = [sb(f"wst{i}", [128, 1200], F32) for i in range(2)]
        gm = sb("gm", [128, 8], F32)
        cw = sb("cw", [128, 12, 5], F32)
        dg = sb("dg", [128, 60, 128], BF16)
        nexpa = sb("nexpa", [128, 8], F32)
        dtbt = sb("dtbt", [128, 8], F32)
        xt = [sb(f"xt{i}", [128, D], F32) for i in range(2)]
        junk = sb("junk", [128, D], BF16)
        ssq = sb("ssq", [128, 1], F32)
        rstd = sb("rstd", [128, 1], F32)
        xn = sb("xn", [128, D], BF16)
        uT = sb("uT", [128, 8, 512], BF16)
        qk_tok = sb("qk_tok", [128, 1024], BF16)
        qkT_st = [sb(f"qkT_st{i}", [128, 8, 128], BF16) for i in range(2)]
        v_st = [sb(f"v_st{i}", [128, 512], BF16) for i in range(2)]
        z_st = [sb(f"z_st{i}", [128, 512], F32) for i in range(2)]
        gb = [sb(f"gb{i}", [128, 16], F32) for i in range(2)]
        tmp8 = sb("tmp8", [128, 8], F32)
        XT = [sb(f"XT{i}", [128, 12, 516], BF16) for i in range(2)]
        Y = [sb(f"Y{i}", [128, 512], F32) for i in range(2)]
        sq = [sb(f"sq{i}", [128, 512], BF16) for i in range(2)]
        rs = [sb(f"rs{i}", [128, 512], F32) for i in range(2)]
        QKn = sb("QKn", [128, 8, 512], BF16)
        VT = sb("VT", [128, 4, 512], BF16)
        KV_st = [sb(f"KV_st{i}", [128, 8, 128], BF16) for i in range(2)]

        pT = Rot([ps(f"pT{i}", [128, 8, 128], BF16) for i in range(2)])
        pG = Rot([ps(f"pG{i}", [128, 512], F32) for i in range(6)])

        P.dma('sp', gm[:], gmix)
        P.dma('sp', cw[:], convw)
        P.dma('sp', nexpa[:], alog)
        P.dma('sp', dtbt[:], dtb)
        P.act(nexpa[:], nexpa[:], AF.Exp)
        P.ts('dve', nexpa[:], nexpa[:], -1.0)
        n = 0
        for k in range(8):
            for pc in range(3):
                st = wst[n % 2]
                P.dma('sp' if n % 2 == 0 else 'pool', st[:], w_in[k * 128:(k + 1) * 128, pc * 1200:(pc + 1) * 1200])
                P.ts('dve' if n % 2 == 0 else 'pool', wbf[:, k, pc * 1200:(pc + 1) * 1200], st[:], gm[:, k:k + 1])
                n += 1
        for c in range(12):
            for j in range(5):
                P.ts('pool', dg[:, c * 5 + j, :], G['ident_f'][:], cw[:, c, j:j + 1])

        for b in range(2):
            P.memset('pool', XT[b][:], 0.0)

        groups = [[0]] + [[1 + 4 * s + i for i in range(4)] for s in range(NS)]
        cnt = [0]

        def evac(out_ap, in_ap):
            e = 'act' if cnt[0] % 2 == 0 else 'dve'
            cnt[0] += 1
            P.copy(e, out_ap, in_ap)

        def conv_group(gi):
            tiles = groups[gi]
            ncol = 128 * len(tiles)
            B = XT[gi % 2]
            for c in range(12):
                pc = pG.next()
                for j in range(5):
                    P.mm(pc[:, 0:ncol], dg[:, c * 5 + j, :], B[:, c, j:j + ncol], start=(j == 0), stop=(j == 4))
                y = Y[c % 2]
                P.act(y[:, 0:ncol], pc[:, 0:ncol], AF.Silu)
                if gi == 0:
                    P.memset('pool', y[:, 0:112], 0.0)
                if c < 8:
                    s_ = sq[c % 2]
                    r_ = rs[c % 2]
                    P.act(s_[:, 0:ncol], y[:, 0:ncol], AF.Square)
                    pn = pG.next()
                    P.mm(pn[:, 0:ncol], G['ones_b'][:], s_[:, 0:ncol])
                    P.act(r_[:, 0:ncol], pn[:, 0:ncol], AF.Sqrt, bias=G['epsb'][:, 0:1])
                    P.recip(r_[:, 0:ncol], r_[:, 0:ncol])
                    qs = (128.0 ** -0.5) if c < 4 else 1.0
                    P.stt(QKn[:, c, 0:ncol], y[:, 0:ncol], qs, r_[:, 0:ncol], ALU.mult, ALU.mult)
                else:
                    P.copy('pool', VT[:, c - 8, 0:ncol], y[:, 0:ncol])
            for i, t in enumerate(tiles):
                P.dma('sp', qkT[t], QKn[:, :, i * 128:(i + 1) * 128])
                pk = pT.next()
                for h in range(4):
                    P.transpose(pk[:, h, :], QKn[:, 4 + h, i * 128:(i + 1) * 128], ident_b[:])
                    P.transpose(pk[:, 4 + h, :], VT[:, h, i * 128:(i + 1) * 128], ident_b[:])
                kvs = KV_st[i % 2]
                evac(kvs[:], pk[:])
                P.dma('pool', kvtok[t], kvs[:])

        for gi, tiles in enumerate(groups):
            if CFG['stop'] >= 0 and gi >= CFG['stop']:
                break
            ncol = 128 * len(tiles)
            for i, t in enumerate(tiles):
                xb = xt[t % 2]
                if t == 0:
                    P.memset('pool', xb[:], 0.0)
                    P.dma('sp', xb[112:128, :], meta)
                else:
                    P.dma('sp', xb[:], x[(t - 1) * 128:t * 128, :])
                P.act(junk[:], xb[:], AF.Square, accum_out=ssq[:])
                rms_rstd(P, G, ssq[:], rstd[:], 128, 1.0 / D)
                P.ts('dve', xn[:], xb[:], rstd[:, 0:1])
                pt = pT.next()
                for c in range(8):
                    P.transpose(pt[:, c, :], xn[:, c * 128:(c + 1) * 128], ident_b[:])
                evac(uT[:, :, i * 128:(i + 1) * 128], pt[:])
                def tokproj(c0, nn):
                    pp = pG.next()
                    for k in range(8):
                        P.mm(pp[:, 0:nn], uT[:, k, i * 128:(i + 1) * 128], wbf[:, k, c0:c0 + nn],
                             start=(k == 0), stop=(k == 7))
                    return pp
                pq = tokproj(0, 512)
                P.act(qk_tok[:, 0:512], pq[:], AF.Copy, scale=0.125)
                pk = tokproj(512, 512)
                P.copy('dve', qk_tok[:, 512:1024], pk[:])
                pt2 = pT.next()
                for c in range(8):
                    P.transpose(pt2[:, c, :], qk_tok[:, c * 128:(c + 1) * 128], ident_b[:])
                qs_ = qkT_st[t % 2]
                evac(qs_[:], pt2[:])
                P.dma('sp', naqT[t], qs_[:, 0:4, :])
                P.dma('sp', nakT[t], qs_[:, 4:8, :])
                pv = tokproj(1024, 512)
                vs_ = v_st[t % 2]
                evac(vs_[:], pv[:])
                P.dma('pool', navg[t].rearrange("p h d -> p (h d)"), vs_[:])
                pz = tokproj(3072, 512)
                zt = z_st[t % 2]
                P.act(zt[:], pz[:], AF.Silu)
                P.dma('pool', zs[t], zt[:])
                pba = tokproj(3584, 16)
                g_ = gb[t % 2]
                P.act(g_[:, 8:16], pba[:, 0:8], AF.Sigmoid)
                P.tt('dve', tmp8[:], pba[:, 8:16], dtbt[:], ALU.add)
                P.act(tmp8[:], tmp8[:], AF.Exp)
                P.act(tmp8[:], tmp8[:], AF.Ln, bias=G['oneb'][:, 0:1])
                P.tt('dve', g_[:, 0:8], tmp8[:], nexpa[:], ALU.mult)
                if t == 0:
                    P.ts('dve', g_[:], g_[:], G['vmask'][:, 0:1])
                P.dma('sp', gbs[t], g_[:])
            B = XT[gi % 2]
            for c in range(12):
                pf = pG.next()
                for k in range(8):
                    P.mm(pf[:, 0:ncol], wbf[:, k, 1536 + c * 128:1536 + (c + 1) * 128], uT[:, k, 0:ncol],
                         start=(k == 0), stop=(k == 7))
                evac(B[:, c, 2:2 + ncol], pf[:, 0:ncol])
            P.memset('pool', B[:, :, 2 + ncol:4 + ncol], 0.0)
            if gi > 0:
                Bp = XT[(gi - 1) % 2]
                pcol = 128 * len(groups[gi - 1])
                P.copy('pool', Bp[:, :, 2 + pcol:4 + pcol], B[:, :, 2:4])
                P.copy('pool', B[:, :, 0:2], Bp[:, :, pcol:2 + pcol])
                conv_group(gi - 1)
            else:
                P.memset('pool', B[:, :, 0:2], 0.0)
        if CFG['stop'] < 0:
            conv_group(len(groups) - 1)
        P.emit(ges)


def phase2(nc, ges, G, ntg, naqT, nakT, navg, badd, ynaT):
    ident_b = G['ident_b']
    cls_of = na_classes(ntg)
    with contextlib.ExitStack() as es:
        def sb(name, shape, dt):
            return es.enter_context(nc.sbuf_tensor("p2_" + name, shape, dt))

        def ps(name, shape, dt):
            return es.enter_context(nc.psum_tensor("p2_" + name, shape, dt))

        P = Prog(nc)
        RING = 8
        kT = [sb(f"kT{i}", [128, 4, 128], BF16) for i in range(RING)]
        Va = [sb(f"Va{i}", [128, 8, 65], BF16) for i in range(RING)]
        kTm = sb("kTm", [128, 4, 128], BF16)
        Vm = sb("Vm", [128, 8, 65], BF16)
        qT = [sb(f"qT{i}", [128, 4, 128], BF16) for i in range(2)]
        qz = [sb(f"qz{i}", [128, 8, 128], BF16) for i in range(2)]
        bst = [sb(f"bst{i}", [128, 1024], F32) for i in range(2)]
        bad = [sb(f"bad{i}", [128, 1024], BF16) for i in range(5)]
        PTt = [sb(f"PT{i}", [128, 1024], BF16) for i in range(6)]
        rsum = sb("rsum", [128, 8], F32)
        yb = sb("yb", [128, 8, 64], BF16)
        yT = [sb(f"yT{i}", [128, 4, 128], BF16) for i in range(2)]

        pS = Rot([[ps(f"pS{i}_{k}", [128, 512], F32) for k in range(2)] for i in range(2)])
        pO = [ps(f"pO{k}", [128, 512], F32) for k in range(2)]
        pY = ps("pY", [128, 4, 128], BF16)

        for i in range(RING):
            P.memset('pool', Va[i][:, :, 64:65], 1.0)
        for i in range(2):
            P.memset('pool', qz[i][:], 0.0)
        P.dma('sp', kTm[:], nakT[0])
        P.dma('sp', Vm[:, :, 0:64], navg[0])
        for h in range(8):
            P.copy('pool', Vm[:, h, 64:65], G['vmask'][:, 0:1])
        loaded = {}
        cur_cls = [-1]

        def load_key(kt):
            slot = kt % RING
            if loaded.get(slot) == kt:
                return slot
            loaded[slot] = kt
            P.dma('sp', kT[slot][:], nakT[kt + 1])
            P.dma('pool', Va[slot][:, :, 0:64], navg[kt + 1])
            return slot

        for j in range(ntg):
            c = cls_of(j)
            if c != cur_cls[0]:
                cur_cls[0] = c
                for si in range(5):
                    st = bst[si % 2]
                    P.dma('sp', st[:], badd[c, si])
                    P.copy('pool', bad[si][:], st[:])
            q = qT[j % 2]
            z = qz[j % 2]
            P.dma('sp', q[:], naqT[j + 1])
            zv = z[:].rearrange("p (pr e) t -> p pr e t", e=2)
            P.copy('pool', zv[0:64, :, 0, :], q[0:64, :, :])
            P.copy('pool', zv[64:128, :, 1, :], q[64:128, :, :])
            kts = na_keytiles(j, ntg)
            slots = [load_key(kt) for kt in kts]
            for si in range(6):
                pS_ = pS.next()
                k_ = kT[slots[si]] if si < 5 else kTm
                for half in range(2):
                    if si < 5:
                        P.mm(pS_[half][:], ident_b[:], bad[si][:, half * 512:(half + 1) * 512],
                             start=True, stop=False)
                    for hl in range(4):
                        h = half * 4 + hl
                        if si < 5:
                            P.mm(pS_[half][:, hl * 128:(hl + 1) * 128], k_[:, h // 2, :], z[:, h, :],
                                 start=False, stop=(hl == 3))
                        else:
                            P.mm(pS_[half][:, hl * 128:(hl + 1) * 128], k_[:, h // 2, :], z[:, h, :],
                                 start=True, stop=True)
                for half in range(2):
                    P.act(PTt[si][:, half * 512:(half + 1) * 512], pS_[half][:], AF.Exp)
            for h in range(8):
                o_ = pO[h // 4][:, (h % 4) * 65:(h % 4) * 65 + 65]
                for si in range(6):
                    v_ = Va[slots[si]] if si < 5 else Vm
                    P.mm(o_, PTt[si][:, h * 128:(h + 1) * 128], v_[:, h, :], start=(si == 0), stop=(si == 5))
            for hf in range(2):
                ov = pO[hf][:, 0:260].rearrange("p (h e) -> p h e", e=65)
                P.recip(rsum[:, hf * 4:(hf + 1) * 4], ov[:, :, 64])
                for hl in range(4):
                    h = hf * 4 + hl
                    P.ts('dve', yb[:, h, :], ov[:, hl, 0:64], rsum[:, h:h + 1])
            ybf = yb[:].rearrange("p h e -> p (h e)")
            for pr in range(4):
                P.transpose(pY[:, pr, :], ybf[:, pr * 128:(pr + 1) * 128], ident_b[:])
            y_ = yT[j % 2]
            P.copy('act', y_[:], pY[:])
            P.dma('pool', ynaT[j + 1], y_[:])
        P.emit(ges)


def phase3(nc, ges, G, ntg, qkT, kvtok, gbs, od):
    NT = ntg + 1
    ident_b = G['ident_b']
    with contextlib.ExitStack() as es:
        def sb(name, shape, dt):
            return es.enter_context(nc.sbuf_tensor("p3_" + name, shape, dt))

        def ps(name, shape, dt):
            return es.enter_context(nc.psum_tensor("p3_" + name, shape, dt))

        P = Prog(nc)
        NB = 2
        qk = [[sb(f"qk{d}{i}", [128, 8, 128], BF16) for i in range(NB)] for d in range(2)]
        kv = [[sb(f"kv{d}{i}", [128, 8, 128], BF16) for i in range(NB)] for d in range(2)]
        gbt = [[sb(f"gb{d}{i}", [128, 16], F32) for i in range(NB)] for d in range(2)]
        sm = [[sb(f"sm{d}{i}", [128, 40], F32) for i in range(NB)] for d in range(2)]
        rhsg = [sb(f"rhsg{d}", [128, 4, 128], F32) for d in range(2)]
        DTs = [sb(f"DTs{d}", [128, 4, 128], F32) for d in range(2)]
        DTi = [sb(f"DTi{d}", [128, 4, 128], F32) for d in range(2)]
        qkTm = [[sb(f"qkTm{d}{i}", [128, 4, 128], BF16) for i in range(NB)] for d in range(2)]
        PTm = [[sb(f"PTm{d}{i}", [128, 4, 128], F32 if i < 2 else BF16) for i in range(3)] for d in range(2)]
        Am = [[sb(f"Am{d}{i}", [128, 4, 128], F32) for i in range(2)] for d in range(2)]
        Bm = [[sb(f"Bm{d}{i}", [128, 4, 128], F32) for i in range(2)] for d in range(2)]
        S = [sb(f"S{d}", [128, 4, 128], F32) for d in range(2)]
        Sb = [sb(f"Sb{d}", [128, 4, 128], BF16) for d in range(2)]
        R = [sb(f"R{d}", [128, 4, 128], BF16) for d in range(2)]
        vn = [sb(f"vn{d}", [128, 4, 128], BF16) for d in range(2)]
        vd = [sb(f"vd{d}", [128, 4, 128], BF16) for d in range(2)]
        tmpo = [sb(f"tmpo{d}", [128, 4, 128], F32) for d in range(2)]
        ob = [[sb(f"ob{d}{i}", [128, 4, 128], F32) for i in range(2)] for d in range(2)]

        pG = Rot([ps(f"pG{i}", [128, 4, 128], F32) for i in range(7)])
        pTr = ps("pTr", [128, 4, 128], F32)

        for d in range(2):
            P.memset('pool', S[d][:], 0.0)
            P.memset('pool', Sb[d][:], 0.0)

        def local(t, d, it):
            b = it % NB
            qk_, kv_, gb_, sm_ = qk[d][b], kv[d][b], gbt[d][b], sm[d][b]
            P.dma('sp', qk_[:], qkT[t])
            P.dma('pool', kv_[:], kvtok[t])
            P.dma('sp', gb_[:], gbs[t])
            gcol = gb_[:, d * 4:d * 4 + 4]
            bcol = gb_[:, 8 + d * 4:8 + d * 4 + 4]
            pg = pG.next()
            pgf = pg[:].rearrange("p h e -> p (h e)")
            P.mm(pgf[:, 0:4], G['u_f'][d][:], gcol, start=True, stop=True)
            P.mm(pgf[:, 4:8], G['ones_f'][:], gcol, start=True, stop=True)
            P.copy('dve', sm_[:, 0:8], pgf[:, 0:8])
            P.ts('dve', sm_[:, 8:12], sm_[:, 0:4], -1.0)
            P.act(sm_[:, 12:16], sm_[:, 0:4], AF.Exp)
            P.ts('dve', sm_[:, 16:20], sm_[:, 12:16], -1.0)
            P.tt('dve', sm_[:, 36:40], sm_[:, 4:8], sm_[:, 0:4], ALU.subtract)
            P.act(sm_[:, 20:24], sm_[:, 36:40], AF.Exp)
            P.act(sm_[:, 24:28], sm_[:, 4:8], AF.Exp)
            P.ts('dve', sm_[:, 28:32], bcol, -1.0)
            P.tt('dve', sm_[:, 32:36], bcol, sm_[:, 20:24], ALU.mult)
            yield
            for h in range(4):
                P.ts('pool', rhsg[d][:, h, :], G['u_f'][d][:], gb_[:, d * 4 + h:d * 4 + h + 1])
            pb = pG.next()
            pbf = pb[:].rearrange("p h e -> p (h e)")
            P.mm(pbf, ident_b[:], G['mk_b'][d][:], start=True, stop=False)
            P.mm(pbf, G['ones_f'][:], rhsg[d][:].rearrange("p h e -> p (h e)"), start=False, stop=True)
            for h in range(4):
                P.act(DTs[d][:, h, :], pb[:, h, :], AF.Exp, bias=sm_[:, 8 + h:9 + h])
            P.tt('pool', DTi[d][:].rearrange("p h e -> p (h e)"), DTs[d][:].rearrange("p h e -> p (h e)"),
                 G['ident4_f'][:], ALU.add)
            yield
            pkk = pG.next()
            pqk = pG.next()
            for h in range(4):
                P.mm(pkk[:, h, :], qk_[:, 4 + h, :], qk_[:, 4 + h, :])
            for h in range(4):
                P.mm(pqk[:, h, :], qk_[:, 4 + h, :], qk_[:, h, :])
            B0 = Bm[d][0]
            for h in range(4):
                P.stt(B0[:, h, :], pkk[:, h, :], sm_[:, 28 + h:29 + h], DTs[d][:, h, :], ALU.mult, ALU.mult)
            P.tt('dve', qkTm[d][b][:], pqk[:], DTi[d][:], ALU.mult)
            pt_i = 0
            PT = PTm[d][pt_i]
            P.tt('pool', PT[:].rearrange("p h e -> p (h e)"), B0[:].rearrange("p h e -> p (h e)"),
                 G['ident4_f'][:], ALU.add)
            yield
            for h in range(4):
                P.transpose(pTr[:, h, :], B0[:, h, :], G['ident_f'][:])
            A0 = Am[d][0]
            P.copy('act', A0[:], pTr[:])
            Ac, Bc = A0, B0
            yield
            for s in range(1, 7):
                An = Am[d][s % 2]
                Bn = Bm[d][s % 2]
                pa = pG.next()
                for h in range(4):
                    P.mm(pa[:, h, :], Bc[:, h, :], Ac[:, h, :])
                if s < 6:
                    pbb = pG.next()
                    for h in range(4):
                        P.mm(pbb[:, h, :], Ac[:, h, :], Bc[:, h, :])
                P.copy('act', An[:], pa[:])
                if s < 6:
                    P.copy('dve', Bn[:], pbb[:])
                yield
                pp = pG.next()
                for h in range(4):
                    P.mm(pp[:, h, :], An[:, h, :], PT[:, h, :])
                if s < 6:
                    PTn = PTm[d][(pt_i + 1) % 2]
                    pt_i = (pt_i + 1) % 2
                else:
                    PTn = PTm[d][2]
                P.tt('dve', PTn[:], pp[:], PT[:], ALU.add)
                PT = PTn
                Ac, Bc = An, Bn
                yield
            return PT

        def scan(t, d, it, PT):
            b = it % NB
            qk_, kv_, gb_, sm_ = qk[d][b], kv[d][b], gbt[d][b], sm[d][b]
            bcol = gb_[:, 8 + d * 4:8 + d * 4 + 4]
            pks = pG.next()
            for h in range(4):
                P.mm(pks[:, h, :], qk_[:, 4 + h, :], Sb[d][:, h, :])
            for h in range(4):
                P.stt(R[d][:, h, :], pks[:, h, :], sm_[:, 16 + h:17 + h], kv_[:, 4 + h, :], ALU.mult, ALU.add)
            yield
            ptr = pG.next()
            for h in range(4):
                P.mm(ptr[:, h, :], PT[:, h, :], R[d][:, h, :])
            for h in range(4):
                P.ts('dve', vn[d][:, h, :], ptr[:, h, :], gb_[:, 8 + d * 4 + h:9 + d * 4 + h])
                P.ts('dve', vd[d][:, h, :], ptr[:, h, :], sm_[:, 32 + h:33 + h])
            yield
            pqs = pG.next()
            for h in range(4):
                P.mm(pqs[:, h, :], qk_[:, h, :], Sb[d][:, h, :])
            for h in range(4):
                P.ts('dve', tmpo[d][:, h, :], pqs[:, h, :], sm_[:, 12 + h:13 + h])
            yield
            po = pG.next()
            for h in range(4):
                P.mm(po[:, h, :], qkTm[d][b][:, h, :], vn[d][:, h, :])
            o_ = ob[d][it % 2]
            P.tt('dve', o_[:], po[:], tmpo[d][:], ALU.add)
            P.dma('sp', od[d, t], o_[:].rearrange("p h e -> p (h e)"))
            yield
            pss = pG.next()
            for h in range(4):
                P.mm(pss[:, h, :], kv_[:, h, :], vd[d][:, h, :])
            for h in range(4):
                P.stt(S[d][:, h, :], S[d][:, h, :], sm_[:, 24 + h:25 + h], pss[:, h, :], ALU.mult, ALU.add)
            P.copy('act', Sb[d][:], S[d][:])

        def interleave(gens):
            res = [None] * len(gens)
            active = list(enumerate(gens))
            while active:
                nxt = []
                for idx, g in active:
                    try:
                        next(g)
                        nxt.append((idx, g))
                    except StopIteration as e:
                        res[idx] = e.value
                active = nxt
            return res

        for n in range(NT):
            tf = n
            tb = NT - 1 - n
            gl = [local(tf, 0, n)]
            if tb >= 1:
                gl.append(local(tb, 1, n))
            pts = interleave(gl)
            gs = [scan(tf, 0, n, pts[0])]
            if tb >= 1:
                gs.append(scan(tb, 1, n, pts[1]))
            interleave(gs)
        P.emit(ges)


def phase4a(nc, ges, G, ntg, x, od, zs, ynaT, ngrep, w_out, h1s):
    ident_b = G['ident_b']
    with contextlib.ExitStack() as es:
        def sb(name, shape, dt):
            return es.enter_context(nc.sbuf_tensor("p4a_" + name, shape, dt))

        def ps(name, shape, dt):
            return es.enter_context(nc.psum_tensor("p4a_" + name, shape, dt))

        P = Prog(nc)
        wo = sb("wo", [128, 8, D], BF16)
        wst = [sb(f"wst{i}", [128, D], F32) for i in range(2)]
        ng = sb("ng", [128, 512], F32)
        xt = [sb(f"xt{i}", [128, D], F32) for i in range(2)]
        of = [sb(f"of{i}", [128, 4, 128], F32) for i in range(2)]
        obw = [sb(f"obw{i}", [128, 4, 128], F32) for i in range(2)]
        zt = [sb(f"zt{i}", [128, 4, 128], F32) for i in range(2)]
        yn = [sb(f"yn{i}", [128, 4, 128], BF16) for i in range(2)]
        junk = sb("junk", [128, 128], BF16)
        ssq = sb("ssq", [128, 4], F32)
        rstd = sb("rstd", [128, 4], F32)
        ydn = sb("ydn", [128, 4, 128], BF16)
        ydT = sb("ydT", [128, 4, 128], BF16)
        h1 = [sb(f"h1{i}", [128, D], F32) for i in range(2)]
        pY = ps("pY", [128, 4, 128], BF16)
        pH = Rot([ps(f"pH{i}", [128, 512], F32) for i in range(4)])

        P.dma('sp', ng[:], ngrep)
        for k in range(8):
            st = wst[k % 2]
            P.dma('sp', st[:], w_out[k * 128:(k + 1) * 128, :])
            P.copy('dve' if k % 2 == 0 else 'pool', wo[:, k, :], st[:])
        for t in range(1, ntg + 1):
            b = t % 2
            P.dma('sp', of[b][:].rearrange("p h e -> p (h e)"), od[0, t])
            P.dma('pool', obw[b][:].rearrange("p h e -> p (h e)"), od[1, t])
            P.dma('sp', zt[b][:].rearrange("p h e -> p (h e)"), zs[t])
            P.dma('pool', yn[b][:], ynaT[t])
            P.dma('sp', xt[b][:], x[(t - 1) * 128:t * 128, :])
            o = of[b]
            P.tt('pool', o[:], o[:], obw[b][:], ALU.add)
            for h in range(4):
                P.act(junk[:], o[:, h, :], AF.Square, accum_out=ssq[:, h:h + 1])
            P.act(rstd[:], ssq[:], AF.Sqrt, bias=G['epsb'][:, 0:1], scale=1.0 / 128)
            P.recip(rstd[:], rstd[:])
            P.tt('pool', zt[b][:].rearrange("p h e -> p (h e)"), zt[b][:].rearrange("p h e -> p (h e)"), ng[:], ALU.mult)
            for h in range(4):
                P.stt(ydn[:, h, :], o[:, h, :], rstd[:, h:h + 1], zt[b][:, h, :], ALU.mult, ALU.mult)
            for h in range(4):
                P.transpose(pY[:, h, :], ydn[:, h, :], ident_b[:])
            P.copy('act', ydT[:], pY[:])
            hb = h1[b]
            for half in range(2):
                ph = pH.next()
                for k in range(8):
                    l = yn[b][:, k, :] if k < 4 else ydT[:, k - 4, :]
                    P.mm(ph[:], l, wo[:, k, half * 512:(half + 1) * 512], start=(k == 0), stop=(k == 7))
                P.tt('dve', hb[:, half * 512:(half + 1) * 512], ph[:], xt[b][:, half * 512:(half + 1) * 512], ALU.add)
            P.dma('pool', h1s[t], hb[:])
        P.emit(ges)


def phase4b(nc, ges, G, ntg, h1s, gffn, gfin, w_gate, w_up, w_down, out):
    ident_b = G['ident_b']
    NS = ntg // 4
    NF = DFF // 128
    with contextlib.ExitStack() as es:
        def sb(name, shape, dt):
            return es.enter_context(nc.sbuf_tensor("p4b_" + name, shape, dt))

        def ps(name, shape, dt):
            return es.enter_context(nc.psum_tensor("p4b_" + name, shape, dt))

        P = Prog(nc)
        wg = sb("wg", [128, 8, DFF], BF16)
        wu = sb("wu", [128, 8, DFF], BF16)
        wd = sb("wd", [128, NF, D], BF16)
        wst = sb("wst", [128, 1408], F32)
        gf = sb("gf", [128, 8], F32)
        gfi = sb("gfi", [128, D], F32)
        h1 = [sb(f"h1{i}", [128, D], F32) for i in range(4)]
        junk = sb("junk", [128, D], BF16)
        ssq = sb("ssq", [128, 1], F32)
        rstd = sb("rstd", [128, 1], F32)
        hn = sb("hn", [128, D], BF16)
        hT = sb("hT", [128, 8, 512], BF16)
        sil = [sb(f"sil{i}", [128, 512], F32) for i in range(2)]
        actT = sb("actT", [128, NF, 512], BF16)
        pT = ps("pT", [128, 8, 128], BF16)
        pG = Rot([ps(f"pG{i}", [128, 512], F32) for i in range(6)])

        P.dma('sp', gf[:], gffn)
        P.dma('sp', gfi[:], gfin)
        n = 0
        for (wsrc, wdst) in ((w_gate, wg), (w_up, wu)):
            for k in range(8):
                for pc in range(2):
                    P.dma('sp' if n % 2 == 0 else 'pool', wst[:], wsrc[k * 128:(k + 1) * 128, pc * 1408:(pc + 1) * 1408])
                    P.ts('dve' if n % 2 == 0 else 'pool', wdst[:, k, pc * 1408:(pc + 1) * 1408], wst[:], gf[:, k:k + 1])
                    n += 1
        for f in range(NF):
            P.dma('sp' if n % 2 == 0 else 'pool', wst[:, 0:D], w_down[f * 128:(f + 1) * 128, :])
            P.copy('dve' if n % 2 == 0 else 'pool', wd[:, f, :], wst[:, 0:D])
            n += 1
        for s in range(NS):
            tiles = [1 + 4 * s + i for i in range(4)]
            for i, t in enumerate(tiles):
                hb = h1[i]
                P.dma('sp', hb[:], h1s[t])
                P.act(junk[:], hb[:], AF.Square, accum_out=ssq[:])
                rms_rstd(P, G, ssq[:], rstd[:], 128, 1.0 / D)
                P.ts('dve', hn[:], hb[:], rstd[:, 0:1])
                for c in range(8):
                    P.transpose(pT[:, c, :], hn[:, c * 128:(c + 1) * 128], ident_b[:])
                P.copy('act', hT[:, :, i * 128:(i + 1) * 128], pT[:])
            for f in range(NF):
                pg = pG.next()
                pu = pG.next()
                for k in range(8):
                    P.mm(pg[:], wg[:, k, f * 128:(f + 1) * 128], hT[:, k, :], start=(k == 0), stop=(k == 7))
                for k in range(8):
                    P.mm(pu[:], wu[:, k, f * 128:(f + 1) * 128], hT[:, k, :], start=(k == 0), stop=(k == 7))
                sl = sil[f % 2]
                P.act(sl[:], pg[:], AF.Silu)
                P.tt('dve', actT[:, f, :], pu[:], sl[:], ALU.mult)
            for i, t in enumerate(tiles):
                hb = h1[i]
                for half in range(2):
                    pd = pG.next()
                    for f in range(NF):
                        P.mm(pd[:], actT[:, f, i * 128:(i + 1) * 128], wd[:, f, half * 512:(half + 1) * 512],
                             start=(f == 0), stop=(f == NF - 1))
                    P.tt('dve', hb[:, half * 512:(half + 1) * 512], pd[:], hb[:, half * 512:(half + 1) * 512], ALU.add)
                P.act(junk[:], hb[:], AF.Square, accum_out=ssq[:])
                rms_rstd(P, G, ssq[:], rstd[:], 128, 1.0 / D)
                P.stt(hb[:], hb[:], rstd[:, 0:1], gfi[:], ALU.mult, ALU.mult)
                P.dma('pool', out[(t - 1) * 128:t * 128, :], hb[:])
        P.emit(ges)


def make_inmap(xb, inp, ntg):
    f = np.float32
    m = {}
    m['x'] = np.ascontiguousarray(xb, dtype=f)
    m['meta'] = np.ascontiguousarray(inp['meta_tokens'], dtype=f)
    m['w_in'] = np.ascontiguousarray(inp['w_in'][0], dtype=f)
    m['w_out'] = np.ascontiguousarray(inp['w_out'][0], dtype=f)
    m['w_gate'] = np.ascontiguousarray(inp['w_gate'][0], dtype=f)
    m['w_up'] = np.ascontiguousarray(inp['w_up'][0], dtype=f)
    m['w_down'] = np.ascontiguousarray(inp['w_down'][0], dtype=f)
    m['gmix'] = np.ascontiguousarray(np.asarray(inp['g_mix'][0], dtype=f).reshape(8, 128).T)
    m['gffn'] = np.ascontiguousarray(np.asarray(inp['g_ffn'][0], dtype=f).reshape(8, 128).T)
    m['gfin'] = np.ascontiguousarray(np.broadcast_to(np.asarray(inp['g_final'], dtype=f)[None, :], (128, D)))
    ng = np.asarray(inp['dn_norm_g'][0], dtype=f)
    m['ngrep'] = np.ascontiguousarray(np.broadcast_to(np.tile(ng, 4)[None, :], (128, 512)))
    cw = np.asarray(inp['dn_conv_w'][0], dtype=f)
    m['convw'] = np.ascontiguousarray(cw.T.reshape(12, 128, 5).transpose(1, 0, 2))
    m['alog'] = np.ascontiguousarray(np.broadcast_to(np.asarray(inp['dn_a_log'][0], dtype=f).reshape(1, 8), (128, 8)))
    m['dtb'] = np.ascontiguousarray(np.broadcast_to(np.asarray(inp['dn_dt_bias'][0], dtype=f).reshape(1, 8), (128, 8)))
    m['badd'] = build_badd(np.asarray(inp['na_rel_bias'][0], dtype=f), ntg)
    return m


_NC_CACHE = {}


def kernel(**inputs):
    x = np.asarray(inputs['x'])
    B, S, _ = x.shape
    ntg = S // 128
    if ntg not in _NC_CACHE:
        _NC_CACHE[ntg] = build(ntg)
    nc = _NC_CACHE[ntg]
    base = make_inmap(x[0], inputs, ntg)
    in_maps = []
    for b in range(B):
        m = dict(base)
        m['x'] = np.ascontiguousarray(x[b], dtype=np.float32)
        in_maps.append(m)
    res = run_bass_kernel_spmd(nc, in_maps, core_ids=list(range(B)))
    return np.stack([np.asarray(r['out'], dtype=np.float32) for r in res.results], axis=0)
```
